# Optimizing a Trainium2 kernel written in Bass

```python
import math
import jax, jax.numpy as jnp
from jax import lax
import numpy as np

D_MODEL = 1024
BATCH = 8
SEQ = 4096
DEPTH = 1

D_MIX = D_MODEL
D_SSM = D_MIX // 2
D_CONV = D_MIX - D_SSM
SSM_GROUP = 16
N_SSM_GROUPS = D_SSM // SSM_GROUP
SSM_STATE = 64
CONV_HEADS = 8
CONV_WIDTH = 3
D_FF = ((8 * D_MODEL // 3 + 127) // 128) * 128
FFN_CONV_WIDTH = 3
N_MOD = 6
D_IN_PROJ = D_SSM + 3 * D_CONV
EPS = 1e-6
DT_MIN = 1e-3
DT_MAX = 1e-1
LAMBDA_RE_MAX = -1e-4

kernel_name = 'hymba_s5_shortconv_convffn_adaln'


def rms_norm(x, g):
    xf = x.astype(jnp.float32)
    y = xf * lax.rsqrt(jnp.mean(xf * xf, axis=-1, keepdims=True) + EPS)
    return (y * g.astype(jnp.float32)).astype(x.dtype)


def head_rms_norm(y, g, n_heads):
    shp = y.shape
    yf = y.astype(jnp.float32).reshape(shp[:-1] + (n_heads, shp[-1] // n_heads))
    yf = yf * lax.rsqrt(jnp.mean(yf * yf, axis=-1, keepdims=True) + EPS)
    return (yf.reshape(shp) * g.astype(jnp.float32)).astype(y.dtype)


def causal_dwconv(x, w):
    k_w = w.shape[0]
    seq = x.shape[1]
    xp = jnp.pad(x, ((0, 0), (k_w - 1, 0), (0, 0)))
    y = xp[:, 0:seq, :] * w[0]
    for k in range(1, k_w):
        y = y + xp[:, k:k + seq, :] * w[k]
    return y


def _s5_binop(e1, e2):
    a1, b1 = e1
    a2, b2 = e2
    return a2 * a1, a2 * b1 + b2


def s5_mixer(u, lam_re, lam_im, log_step, b_re, b_im, c_re, c_im, d_skip, glu_w, glu_b):
    bsz, seq, _ = u.shape
    uf = u.astype(jnp.float32)
    ug = uf.reshape(bsz, seq, N_SSM_GROUPS, SSM_GROUP)
    lam = lax.complex(jnp.minimum(lam_re.astype(jnp.float32), LAMBDA_RE_MAX),
                      lam_im.astype(jnp.float32))
    step = jnp.exp(log_step.astype(jnp.float32))[:, None]
    lam_bar = jnp.exp(lam * step)
    b_c = lax.complex(b_re.astype(jnp.float32), b_im.astype(jnp.float32))
    b_bar = ((lam_bar - 1.0) / lam)[..., None] * b_c
    bu = jnp.einsum('blgh,gph->blgp', ug.astype(jnp.complex64), b_bar)
    a = jnp.broadcast_to(lam_bar, (1, seq) + lam_bar.shape)
    _, states = lax.associative_scan(_s5_binop, (a, bu), axis=1)
    c_c = lax.complex(c_re.astype(jnp.float32), c_im.astype(jnp.float32))
    y = jnp.einsum('blgp,ghp->blgh', states, c_c).real.reshape(bsz, seq, D_SSM)
    y = y + d_skip.astype(jnp.float32) * uf
    z = jax.nn.gelu(y)
    z = z * jax.nn.sigmoid(z @ glu_w.astype(jnp.float32) + glu_b.astype(jnp.float32))
    return z.astype(u.dtype)


def short_conv_mixer(bg, cg, v, conv_w):
    return bg * causal_dwconv(cg * v, conv_w)


def conv_ffn(h, w_up, ffn_conv_w, w_down):
    hid = causal_dwconv(h @ w_up, ffn_conv_w)
    a, v = jnp.split(hid, 2, axis=-1)
    return (jax.nn.silu(a) * v) @ w_down


def setup_inputs(seed: int = 0) -> dict:
    key = jax.random.key(seed)
    ks = jax.random.split(key, 26)
    f32 = jnp.float32

    def nrm(k, shape, s):
        return jax.random.normal(k, shape, f32) * s

    nl = DEPTH
    g_, p_, h_ = N_SSM_GROUPS, SSM_STATE, SSM_GROUP
    n_idx = jnp.arange(SSM_STATE, dtype=f32)
    return {
        'x': nrm(ks[0], (BATCH, SEQ, D_MODEL), 1.0),
        'c': nrm(ks[1], (BATCH, D_MODEL), 1.0),
        'w_ada': nrm(ks[2], (nl, D_MODEL, N_MOD * D_MODEL), 0.5 * D_MODEL ** -0.5),
        'b_ada': nrm(ks[3], (nl, N_MOD * D_MODEL), 0.02),
        'g_pre_mix': 1.0 + nrm(ks[4], (nl, D_MODEL), 0.02),
        'g_post_mix': 1.0 + nrm(ks[5], (nl, D_MODEL), 0.02),
        'w_in': nrm(ks[6], (nl, D_MODEL, D_IN_PROJ), D_MODEL ** -0.5),
        'ssm_lam_re': -0.5 + nrm(ks[7], (nl, g_, p_), 0.01),
        'ssm_lam_im': math.pi * n_idx + nrm(ks[8], (nl, g_, p_), 0.01),
        'ssm_log_step': jax.random.uniform(ks[9], (nl, g_), f32, math.log(DT_MIN), math.log(DT_MAX)),
        'ssm_b_re': nrm(ks[10], (nl, g_, p_, h_), (2 * h_) ** -0.5),
        'ssm_b_im': nrm(ks[11], (nl, g_, p_, h_), (2 * h_) ** -0.5),
        'ssm_c_re': nrm(ks[12], (nl, g_, h_, p_), p_ ** -0.5),
        'ssm_c_im': nrm(ks[13], (nl, g_, h_, p_), p_ ** -0.5),
        'ssm_d': nrm(ks[14], (nl, D_SSM), 1.0),
        'glu_w': nrm(ks[15], (nl, D_SSM, D_SSM), D_SSM ** -0.5),
        'glu_b': nrm(ks[16], (nl, D_SSM), 0.02),
        'g_out_ssm': 1.0 + nrm(ks[17], (nl, D_SSM), 0.02),
        'conv_w': nrm(ks[18], (nl, CONV_WIDTH, D_CONV), CONV_WIDTH ** -0.5),
        'g_out_conv': 1.0 + nrm(ks[19], (nl, D_CONV), 0.02),
        'w_out': nrm(ks[20], (nl, D_MIX, D_MODEL), D_MIX ** -0.5),
        'g_pre_ffn': 1.0 + nrm(ks[21], (nl, D_MODEL), 0.02),
        'g_post_ffn': 1.0 + nrm(ks[22], (nl, D_MODEL), 0.02),
        'w_up': nrm(ks[23], (nl, D_MODEL, 2 * D_FF), D_MODEL ** -0.5),
        'ffn_conv_w': nrm(ks[24], (nl, FFN_CONV_WIDTH, 2 * D_FF), FFN_CONV_WIDTH ** -0.5),
        'w_down': nrm(ks[25], (nl, D_FF, D_MODEL), D_FF ** -0.5),
    }


def reference(x, c, w_ada, b_ada, g_pre_mix, g_post_mix, w_in, ssm_lam_re, ssm_lam_im, ssm_log_step,
              ssm_b_re, ssm_b_im, ssm_c_re, ssm_c_im, ssm_d, glu_w, glu_b, g_out_ssm, conv_w, g_out_conv,
              w_out, g_pre_ffn, g_post_ffn, w_up, ffn_conv_w, w_down):
    c_act = jax.nn.silu(c)
    for i in range(DEPTH):
        mod = (c_act @ w_ada[i] + b_ada[i])[:, None, :]
        sh1, sc1, gt1, sh2, sc2, gt2 = jnp.split(mod, N_MOD, axis=-1)

        h = rms_norm(x, g_pre_mix[i]) * (1.0 + sc1) + sh1
        proj = h @ w_in[i]
        u = proj[..., :D_SSM]
        bg, cg, v = jnp.split(proj[..., D_SSM:], 3, axis=-1)
        y_a = s5_mixer(u, ssm_lam_re[i], ssm_lam_im[i], ssm_log_step[i], ssm_b_re[i], ssm_b_im[i],
                       ssm_c_re[i], ssm_c_im[i], ssm_d[i], glu_w[i], glu_b[i])
        y_b = short_conv_mixer(bg, cg, v, conv_w[i])
        y = jnp.concatenate([head_rms_norm(y_a, g_out_ssm[i], N_SSM_GROUPS),
                             head_rms_norm(y_b, g_out_conv[i], CONV_HEADS)], axis=-1)
        x = x + gt1 * rms_norm(y @ w_out[i], g_post_mix[i])

        h = rms_norm(x, g_pre_ffn[i]) * (1.0 + sc2) + sh2
        x = x + gt2 * rms_norm(conv_ffn(h, w_up[i], ffn_conv_w[i], w_down[i]), g_post_ffn[i])
    return x
```

```python
import contextlib
import math
import numpy as np
import concourse.bass as bass
import concourse.mybir as mybir
from concourse.bass_utils import run_bass_kernel_spmd

F32 = mybir.dt.float32
BF16 = mybir.dt.bfloat16
I32 = mybir.dt.int32
ALU = mybir.AluOpType
AF = mybir.ActivationFunctionType

NDSEM = 28
NSWSEM = 8
L = 4096
D = 1024
TT = 512
NT = L // TT
DFF = 2816
NF = DFF // 128
EPS = 1e-6

DEBUG = False


class Sched:
    ENG = ("pe", "act", "dve", "pool", "sp")

    def __init__(self, nc):
        self.nc = nc
        self.q = {e: [] for e in self.ENG}
        self.cnt = {e: 0 for e in self.ENG}
        self.waited = {e: {} for e in self.ENG}
        self.res = {}
        self.ndma = 0
        self.ndma_sw = 0
        self.dma_last = [None] * NDSEM

    def _deps(self, reads, writes):
        deps = []
        for r in reads:
            st = self.res.get(r)
            if st and st["w"] is not None:
                deps.append(st["w"])
        for w in writes:
            st = self.res.get(w)
            if st:
                if st["w"] is not None:
                    deps.append(st["w"])
                deps.extend(st["r"])
        return deps

    def op(self, eng, fn, reads=(), writes=(), dma=False):
        deps = self._deps(reads, writes)
        if dma:
            if eng == "pool":
                k = self.ndma_sw % NSWSEM
                j = NDSEM - NSWSEM + k
                val = 16 * (self.ndma_sw // NSWSEM + 1)
                self.ndma_sw += 1
            else:
                j = self.ndma % (NDSEM - NSWSEM)
                val = 16 * (self.ndma // (NDSEM - NSWSEM) + 1)
                self.ndma += 1
            if self.dma_last[j] is not None:
                deps.append(self.dma_last[j])
            tok = ("d%d" % j, val)
            self.dma_last[j] = tok
        else:
            self.cnt[eng] += 1
            tok = (eng, self.cnt[eng])
        wd = self.waited[eng]
        m = {}
        for (s, v) in deps:
            if s == eng and eng == "pe":
                continue
            if wd.get(s, 0) >= v:
                continue
            m[s] = max(m.get(s, 0), v)
        for s, v in m.items():
            wd[s] = v
        self.q[eng].append((list(m.items()), fn, tok, dma))
        for r in reads:
            st = self.res.setdefault(r, {"w": None, "r": []})
            st["r"].append(tok)
            if len(st["r"]) > 64:
                st["r"] = st["r"][-64:] if False else st["r"]
        for w in writes:
            self.res[w] = {"w": tok, "r": []}
        return tok

    def barrier(self):
        toks = [(e, self.cnt[e]) for e in self.ENG if self.cnt[e] > 0]
        toks += [t for t in self.dma_last if t is not None]
        for e in self.ENG:
            m = {}
            wd = self.waited[e]
            for s, v in toks:
                if wd.get(s, 0) >= v:
                    continue
                m[s] = v
                wd[s] = v
            if m:
                self.q[e].append((list(m.items()), None, None, False))
        self.res = {}

    def emit(self):
        nc = self.nc
        with contextlib.ExitStack() as es:
            sems = {}
            for e in self.ENG:
                sems[e] = es.enter_context(nc.semaphore("s_" + e))
            for j in range(NDSEM):
                sems["d%d" % j] = es.enter_context(nc.semaphore("s_d%d" % j))
            block = es.enter_context(nc.Block())

            def run(engname, eh):
                for waits, fn, tok, dma in self.q[engname]:
                    for s, v in waits:
                        eh.wait_ge(sems[s], v)
                    if fn is None:
                        continue
                    ins = fn(eh)
                    ins.then_inc(sems[tok[0]], 16 if dma else 1)

            @block.tensor
            def _(e):
                run("pe", e)

            @block.scalar
            def _(e):
                run("act", e)

            @block.vector
            def _(e):
                run("dve", e)

            @block.gpsimd
            def _(e):
                run("pool", e)

            @block.sync
            def _(e):
                run("sp", e)


PP_FIELDS = [("gpm", 8), ("gqm", 8), ("gpf", 8), ("gqf", 8), ("bada", 48), ("ssd", 4), ("glub", 4),
             ("gos", 4), ("goc", 4), ("cw", 12), ("fcw", 132), ("c", 8), ("sgn", 1), ("nsg", 1),
             ("eps", 1), ("hpi", 1), ("kv", 9), ("tpn", 1), ("zero", 1)]
PP_OFF = {}
_o = 0
for _n, _w in PP_FIELDS:
    PP_OFF[_n] = (_o, _w)
    _o += _w
NPP = _o


def _colmajor(v, n):
    return np.ascontiguousarray(np.asarray(v, np.float32).reshape(n, 128).T)


def pack_pp(inp, b):
    pp = np.zeros((128, NPP), np.float32)

    def put(name, arr):
        o, w = PP_OFF[name]
        pp[:, o:o + w] = np.asarray(arr, np.float32).reshape(128, w)

    put("gpm", _colmajor(inp["g_pre_mix"][0], 8))
    put("gqm", _colmajor(inp["g_post_mix"][0], 8))
    put("gpf", _colmajor(inp["g_pre_ffn"][0], 8))
    put("gqf", _colmajor(inp["g_post_ffn"][0], 8))
    put("bada", _colmajor(inp["b_ada"][0], 48))
    put("ssd", _colmajor(inp["ssm_d"][0], 4))
    put("glub", _colmajor(inp["glu_b"][0], 4))
    put("gos", _colmajor(inp["g_out_ssm"][0], 4))
    put("goc", _colmajor(inp["g_out_conv"][0], 4))
    cw = np.stack([_colmajor(inp["conv_w"][0][k], 4) for k in range(3)], axis=1)
    put("cw", cw)
    fcw = np.stack([_colmajor(inp["ffn_conv_w"][0][k], 44) for k in range(3)], axis=1)
    put("fcw", fcw)
    put("c", _colmajor(inp["c"][b], 8))
    sgn = np.concatenate([-np.ones(64), np.ones(64)]).astype(np.float32)
    put("sgn", sgn)
    put("nsg", -sgn)
    put("eps", np.full(128, EPS, np.float32))
    put("hpi", np.full(128, math.pi / 2, np.float32))
    put("kv", np.tile(np.arange(9, dtype=np.float32), (128, 1)))
    put("tpn", (-sgn) * np.float32(2 * math.pi))
    return pp


def pack_ssm(inp):
    lre = np.asarray(inp["ssm_lam_re"][0], np.float32).T
    lim = np.asarray(inp["ssm_lam_im"][0], np.float32).T
    ls = np.tile(np.asarray(inp["ssm_log_step"][0], np.float32)[None, :], (64, 1))
    sp1 = np.stack([np.concatenate([a, a], 0) for a in (lre, lim, ls)], axis=1)
    bre = np.transpose(np.asarray(inp["ssm_b_re"][0], np.float32), (1, 0, 2))
    bim = np.transpose(np.asarray(inp["ssm_b_im"][0], np.float32), (1, 0, 2))
    cre = np.transpose(np.asarray(inp["ssm_c_re"][0], np.float32), (2, 0, 1))
    cim = np.transpose(np.asarray(inp["ssm_c_im"][0], np.float32), (2, 0, 1))
    BC = np.stack([np.concatenate([bre, bim], 0), np.concatenate([bim, bre], 0),
                   np.concatenate([cre, cim], 0), np.concatenate([cim, cre], 0)], axis=1)
    dg = np.ascontiguousarray(np.asarray(inp["ssm_d"][0], np.float32).reshape(32, 16).T)
    return (np.ascontiguousarray(sp1, dtype=np.float32), np.ascontiguousarray(BC, dtype=np.float32), dg)


def make_consts():
    ident = np.eye(128, dtype=np.float32)
    ones = np.ones((128, 128), np.float32)
    bd16 = np.kron(np.eye(8, dtype=np.float32), np.ones((16, 16), np.float32))
    bd64 = np.kron(np.eye(2, dtype=np.float32), np.ones((64, 64), np.float32))
    cst = np.stack([ident, ones, bd16, bd64], axis=1)
    iota = np.tile(np.arange(512, dtype=np.float32)[None, :], (128, 1))
    return np.ascontiguousarray(cst), iota


def build_nc(phases=("prep", "a1", "ssm", "a2", "ffn"), dbg=(), ntiles=NT, upto=99):
    nc = bass.Bass("TRN2", target_bir_lowering=False)
    dt_ = lambda name, shape, kind="ExternalInput", dt=F32: nc.dram_tensor(name, list(shape), dt, kind=kind).ap()
    xT = dt_("xT", [D, L])
    outT = dt_("outT", [D, L], "ExternalOutput")
    x1s = outT
    w_ada = dt_("w_ada", [D, 6 * D])
    w_in = dt_("w_in", [D, 2048])
    glu_w = dt_("glu_w", [512, 512])
    w_out = dt_("w_out", [D, D])
    w_up = dt_("w_up", [D, 2 * DFF])
    w_down = dt_("w_down", [DFF, D])
    pp_d = dt_("pp", [128, NPP])
    sp1_d = dt_("sp1", [128, 3, 32])
    BC_d = dt_("BC", [128, 4, 32, 16])
    dg_d = dt_("dg", [16, 32])
    cst_d = dt_("cst", [128, 4, 128])
    iota_d = dt_("iota", [128, 512])
    dbg_t = {}
    for name, shape, ddt in dbg:
        dbg_t[name] = dt_("dbg_" + name, shape, "ExternalOutput", dt=ddt)

    es = contextlib.ExitStack()
    with es:
        ARENA_B = 200 * 1024
        arena = es.enter_context(nc.sbuf_tensor("arena", [128, ARENA_B // 4], F32))
        sb = lambda name, shape, dt=F32: es.enter_context(nc.sbuf_tensor("s_" + name, list(shape), dt))
        cstb = sb("cstb", [128, 4, 128], BF16)
        id32 = sb("id32", [128, 128], F32)
        iota = sb("iota", [128, 512], F32)
        pp = sb("pp", [128, NPP], F32)
        modT = sb("modT", [128, 48], F32)
        der = sb("der", [128, 32], F32)
        cs = sb("cs", [128, 8], F32)
        ssc = sb("ssc", [128, 4, 32], F32)
        halo = sb("halo", [128, 44, 2], F32)
        dgs = sb("dgs", [16, 32], F32)
        ps = [es.enter_context(nc.psum_tensor("ps%d" % i, [128, 512], F32)) for i in range(8)]

        def carve(off, shape, dt=F32):
            esz = 4 if dt in (F32, I32) else 2
            n = 1
            for s in shape[1:]:
                n *= s
            nb = n * esz
            assert off % 4 == 0 and off + nb <= ARENA_B, (off, shape, nb)
            w0, w1 = off // 4, (off + nb + 3) // 4
            ap = arena[:, w0:w1]
            if dt != F32:
                ap = ap.bitcast(dt)
                if esz == 2 and (nb % 4):
                    ap = ap[:, 0:n]
            if len(shape) == 3:
                ap = ap.rearrange("p (a b) -> p a b", b=shape[2])
            elif len(shape) == 4:
                ap = ap.rearrange("p (a b c) -> p a b c", b=shape[2], c=shape[3])
            return ap

        KB = 1024
        S = Sched(nc)
        ppc = lambda name, i=0, n=1: pp[:, PP_OFF[name][0] + i:PP_OFF[name][0] + i + n]
        ident_b = cstb[:, 0, :]
        ones_b = cstb[:, 1, :]
        bd16_b = cstb[:, 2, :]
        bd64_b = cstb[:, 3, :]

        psrr = [0]

        def nps():
            i = psrr[0] % 6
            psrr[0] += 1
            return i

        strr = [0]

        def nst():
            i = 6 + strr[0] % 2
            strr[0] += 1
            return i

        def dma(eng, out, in_, reads=(), writes=()):
            return S.op(eng, lambda e: e.dma_start(out=out, in_=in_), reads=reads, writes=writes, dma=True)

        def mmg(psi, pairs, reads, cols=None, writes=None):
            out = ps[psi][:] if cols is None else cols

            def fn(e):
                n = len(pairs)
                ins = None
                for i, (l, r) in enumerate(pairs):
                    ins = e.matmul(out, lhsT=l, rhs=r, start=(i == 0), stop=(i == n - 1))
                return ins
            return S.op("pe", fn, reads=reads, writes=[("ps", psi)] if writes is None else writes)

        dma("sp", pp[:], pp_d, writes=["pp"])
        dma("pool", cstb[:], cst_d, writes=["cstb"])
        dma("sp", id32[:], cst_d[:, 0, :], writes=["id32"])
        dma("sp", iota[:], iota_d, writes=["iota"])
        dma("sp", dgs[:], dg_d, writes=["dgs"])
        S.op("pool", lambda e: e.memset(halo[:], 0.0), writes=["halo"])

        if "prep" in phases:
            S.op("act", lambda e: e.activation(out=cs[:], in_=ppc("c", 0, 8), func=AF.Silu),
                 reads=["pp"], writes=["cs"])
            wa = [carve(40 * KB + i * 24 * KB, [128, 8, 768]) for i in range(2)]
            wav = w_ada.rearrange("(kc p) m -> p kc m", p=128)
            pm = nst()
            for pc in range(8):
                buf = wa[pc % 2]
                for kc in range(8):
                    dma("sp", buf[:, kc, :], wav[:, kc, pc * 768:(pc + 1) * 768], writes=[("wa", pc % 2, kc)])
                for mb in range(6):
                    m = pc * 6 + mb
                    mmg(pm, [(buf[:, kc, mb * 128:(mb + 1) * 128], cs[:, kc:kc + 1]) for kc in range(8)],
                        reads=[("wa", pc % 2, kc) for kc in range(8)] + ["cs"], cols=ps[pm][:, m:m + 1])
            S.op("dve", lambda e: e.tensor_tensor(out=modT[:], in0=ps[pm][:, 0:48], in1=ppc("bada", 0, 48), op=ALU.add),
                 reads=[("ps", pm), "pp"], writes=["modT"])
            S.op("dve", lambda e: e.tensor_scalar(out=der[:, 0:8], in0=modT[:, 8:16], scalar1=1.0, scalar2=None, op0=ALU.add),
                 reads=["modT"], writes=["der0"])
            S.op("dve", lambda e: e.tensor_tensor(out=der[:, 0:8], in0=der[:, 0:8], in1=ppc("gpm", 0, 8), op=ALU.mult),
                 reads=["der0", "pp"], writes=["der0"])
            S.op("dve", lambda e: e.tensor_tensor(out=der[:, 8:16], in0=modT[:, 16:24], in1=ppc("gqm", 0, 8), op=ALU.mult),
                 reads=["modT", "pp"], writes=["der1"])
            S.op("dve", lambda e: e.tensor_scalar(out=der[:, 16:24], in0=modT[:, 32:40], scalar1=1.0, scalar2=None, op0=ALU.add),
                 reads=["modT"], writes=["der2"])
            S.op("dve", lambda e: e.tensor_tensor(out=der[:, 16:24], in0=der[:, 16:24], in1=ppc("gpf", 0, 8), op=ALU.mult),
                 reads=["der2", "pp"], writes=["der2"])
            S.op("dve", lambda e: e.tensor_tensor(out=der[:, 24:32], in0=modT[:, 40:48], in1=ppc("gqf", 0, 8), op=ALU.mult),
                 reads=["modT", "pp"], writes=["der3"])
            if "modT" in dbg_t:
                dma("sp", dbg_t["modT"], modT[:], reads=["modT"])
        DER = ["der0", "der1", "der2", "der3", "modT", "pp"]

        def dve_tt(out, a, b, op, reads, writes, eng="dve"):
            return S.op(eng, lambda e: e.tensor_tensor(out=out, in0=a, in1=b, op=op), reads=reads, writes=writes)

        def dve_ts(out, a, s1, op0, reads, writes, s2=None, op1=None, eng="dve"):
            if op1 is None:
                return S.op(eng, lambda e: e.tensor_scalar(out=out, in0=a, scalar1=s1, scalar2=None, op0=op0),
                            reads=reads, writes=writes)
            return S.op(eng, lambda e: e.tensor_scalar(out=out, in0=a, scalar1=s1, scalar2=s2, op0=op0, op1=op1),
                        reads=reads, writes=writes)

        def dve_stt(out, a, s, b, op0, op1, reads, writes, eng="dve"):
            return S.op("dve", lambda e: e.scalar_tensor_tensor(out=out, in0=a, scalar=s, in1=b, op0=op0, op1=op1),
                        reads=reads, writes=writes)

        def act(out, in_, func, reads, writes, scale=1.0, bias=None):
            if bias is None:
                return S.op("act", lambda e: e.activation(out=out, in_=in_, func=func, scale=scale),
                            reads=reads, writes=writes)
            return S.op("act", lambda e: e.activation(out=out, in_=in_, func=func, scale=scale, bias=bias),
                        reads=reads, writes=writes)

        def rstd_from_ps(psi, inv_n, srt, rstd, tag):
            act(rstd, ps[psi][:], AF.Sqrt, reads=[("ps", psi), "pp"], writes=[tag + "rstd"], scale=inv_n, bias=ppc("eps"))
            S.op("dve", lambda e: e.reciprocal(out=rstd, in_=rstd), reads=[tag + "rstd"], writes=[tag + "rstd"])

        MATS = 0
        M_all = carve(MATS + 0 * KB, [128, 32, 128], BF16)
        WE_all = carve(MATS + 8 * KB, [128, 32, 128], BF16)
        WEs_all = carve(MATS + 16 * KB, [128, 32, 128], BF16)
        WY1_all = carve(MATS + 24 * KB, [128, 32, 128], BF16)
        WY2_all = carve(MATS + 32 * KB, [128, 32, 128], BF16)
        if "prep" in phases:
            P0 = 100 * KB
            BCs = carve(P0, [128, 4, 32, 16])
            sp1 = carve(P0 + 8 * KB, [128, 3, 32])
            sc_ = [carve(P0 + 9 * KB + i * 128, [128, 32]) for i in range(24)]
            pw = [carve(P0 + 12 * KB + i * 1152, [128, 32, 9]) for i in range(11)]
            Bb = carve(P0 + 26 * KB, [128, 32, 16])
            Bbs = carve(P0 + 28 * KB, [128, 32, 16])
            Gt = carve(P0 + 30 * KB, [128, 32, 8, 16])
            Rt = carve(P0 + 46 * KB, [128, 32, 9, 16])
            R2 = carve(P0 + 64 * KB, [128, 32, 9, 16])
            tmpA = carve(P0 + 82 * KB, [128, 32, 16])
            tmpB = carve(P0 + 84 * KB, [128, 32, 16])
            KTs = carve(P0 + 86 * KB, [128, 32, 128], BF16)
            pwi = carve(P0 + 94 * KB, [128, 32, 9], I32)
            sci = carve(P0 + 96 * KB, [128, 32], I32)
            dma("sp", BCs[:], BC_d, writes=["BCs"])
            dma("sp", sp1[:], sp1_d, writes=["sp1"])
            S.op("pool", lambda e: e.memset(M_all[:], 0.0), writes=["M_all"])
            k_ = [0]

            def sc_new():
                k_[0] += 1
                return sc_[k_[0] - 1], ("sc", k_[0] - 1)

            def tt(a, b, op, ra, rb, eng="dve"):
                o, ro = sc_new()
                dve_tt(o, a, b, op, reads=[ra, rb], writes=[ro], eng=eng)
                return o, ro

            lre, lim, lst = sp1[:, 0, :], sp1[:, 1, :], sp1[:, 2, :]
            re_, rre = sc_new()
            dve_ts(re_, lre, -1e-4, ALU.min, reads=["sp1"], writes=[rre])
            dtt, rdt = sc_new()
            act(dtt, lst, AF.Exp, reads=["sp1"], writes=[rdt])
            a_, ra = tt(re_, dtt, ALU.mult, rre, rdt)
            th, rth = tt(lim, dtt, ALU.mult, "sp1", rdt)
            t_, rt = sc_new()
            dve_ts(t_, a_, 1.0 / 6, ALU.mult, reads=[ra], writes=[rt], s2=1.0, op1=ALU.add)
            for kk in (5, 4, 3, 2):
                dve_tt(t_, t_, a_, ALU.mult, reads=[rt, ra], writes=[rt])
                dve_ts(t_, t_, 1.0 / kk, ALU.mult, reads=[rt], writes=[rt], s2=1.0, op1=ALU.add)
            em1, rem1 = tt(t_, a_, ALU.mult, rt, ra)
            mag1, rmag1 = sc_new()
            dve_ts(mag1, em1, 1.0, ALU.add, reads=[rem1], writes=[rmag1])
            magk, ck, sk, pwr, pwi_, ang, angr, ang2 = pw[0:8]
            S.op("pool", lambda e: e.memset(magk[:, :, 0:1], 1.0), writes=["magk"])
            for k in range(1, 9):
                dve_tt(magk[:, :, k], magk[:, :, k - 1], mag1, ALU.mult, reads=["magk", rmag1], writes=["magk"])
            tu, rtu = sc_new()
            dve_ts(tu, th, 1.0 / (2 * math.pi), ALU.mult, reads=[rth], writes=[rtu])
            kvb = ppc("kv", 0, 9).unsqueeze(1).broadcast_to([128, 32, 9])
            dve_tt(ang[:], tu.unsqueeze(2).broadcast_to([128, 32, 9]), kvb, ALU.mult, reads=[rtu, "pp"], writes=["ang"])

            def reduce_turns(x, rx, xi):
                S.op("dve", lambda e: e.tensor_copy(out=xi, in_=x), reads=[rx], writes=[("i", rx)])
                dve_tt(x, x, xi, ALU.subtract, reads=[rx, ("i", rx)], writes=[rx])

            reduce_turns(ang[:], "ang", pwi[:])
            act(sk[:], ang[:], AF.Sin, reads=["ang"], writes=["sk"], scale=2 * math.pi)
            act(ang2[:], ang[:], AF.Abs, reads=["ang"], writes=["ang2"])
            act(ck[:], ang2[:], AF.Sin, reads=["ang2", "pp"], writes=["ck"], scale=-2 * math.pi, bias=ppc("hpi"))
            dve_tt(pwr[:], magk[:], ck[:], ALU.mult, reads=["magk", "ck"], writes=["pwr"])
            dve_tt(pwi_[:], magk[:], sk[:], ALU.mult, reads=["magk", "sk"], writes=["pwi"])
            hh, rhh = sc_new()
            dve_ts(hh, tu, 0.5, ALU.mult, reads=[rtu], writes=[rhh])
            reduce_turns(hh, rhh, sci[:])
            shf, rshf = sc_new()
            act(shf, hh, AF.Sin, reads=[rhh], writes=[rshf], scale=2 * math.pi)
            nr, rnr = tt(em1, ck[:, :, 1], ALU.mult, rem1, "ck")
            s2_, rs2 = tt(shf, shf, ALU.mult, rshf, rshf)
            dve_stt(nr, s2_, -2.0, nr, ALU.mult, ALU.add, reads=[rs2, rnr], writes=[rnr])
            ni, rni = tt(mag1, sk[:, :, 1], ALU.mult, rmag1, "sk")
            den, rden = tt(re_, re_, ALU.mult, rre, rre)
            t2, rt2 = tt(lim, lim, ALU.mult, "sp1", "sp1")
            dve_tt(den, den, t2, ALU.add, reads=[rden, rt2], writes=[rden])
            S.op("dve", lambda e: e.reciprocal(out=den, in_=den), reads=[rden], writes=[rden])
            cr, rcr = tt(nr, re_, ALU.mult, rnr, rre)
            t3, rt3 = tt(ni, lim, ALU.mult, rni, "sp1")
            dve_tt(cr, cr, t3, ALU.add, reads=[rcr, rt3], writes=[rcr])
            dve_tt(cr, cr, den, ALU.mult, reads=[rcr, rden], writes=[rcr])
            ci, rci = tt(ni, re_, ALU.mult, rni, rre)
            t4, rt4 = tt(nr, lim, ALU.mult, rnr, "sp1")
            dve_tt(ci, ci, t4, ALU.subtract, reads=[rci, rt4], writes=[rci])
            dve_tt(ci, ci, den, ALU.mult, reads=[rci, rden], writes=[rci])
            cis, rcis = sc_new()
            dve_ts(cis, ci, ppc("sgn"), ALU.mult, reads=[rci, "pp"], writes=[rcis])
            cin, rcin = sc_new()
            dve_ts(cin, ci, ppc("nsg"), ALU.mult, reads=[rci, "pp"], writes=[rcin])
            bc16 = lambda a: a.unsqueeze(2).broadcast_to([128, 32, 16])
            Bc, Bsw, Cc, Csw = BCs[:, 0], BCs[:, 1], BCs[:, 2], BCs[:, 3]
            dve_tt(Bb[:], Bc, bc16(cr), ALU.mult, reads=["BCs", rcr], writes=["Bb"])
            dve_tt(tmpA[:], Bsw, bc16(cis), ALU.mult, reads=["BCs", rcis], writes=["tmpA"])
            dve_tt(Bb[:], Bb[:], tmpA[:], ALU.add, reads=["Bb", "tmpA"], writes=["Bb"])
            dve_tt(Bbs[:], Bsw, bc16(cr), ALU.mult, reads=["BCs", rcr], writes=["Bbs"])
            dve_tt(tmpB[:], Bc, bc16(cin), ALU.mult, reads=["BCs", rcin], writes=["tmpB"])
            dve_tt(Bbs[:], Bbs[:], tmpB[:], ALU.add, reads=["Bbs", "tmpB"], writes=["Bbs"])
            pis, rpis = pw[8], "pis"
            dve_ts(pis[:], pwi_[:], ppc("sgn"), ALU.mult, reads=["pwi", "pp"], writes=[rpis])
            for ip in range(8):
                k = 7 - ip
                dve_tt(Gt[:, :, ip, :], Bb[:], bc16(pwr[:, :, k]), ALU.mult, reads=["Bb", "pwr"], writes=[("G", ip)])
                dve_tt(tmpA[:], Bbs[:], bc16(pis[:, :, k]), ALU.mult, reads=["Bbs", rpis], writes=["tmpA"], eng="pool")
                dve_tt(Gt[:, :, ip, :], Gt[:, :, ip, :], tmpA[:], ALU.add, reads=[("G", ip), "tmpA"], writes=[("G", ip)])
            prn, rprn = pw[9], "prn"
            dve_ts(prn[:], pwr[:], ppc("nsg"), ALU.mult, reads=["pwr", "pp"], writes=[rprn])
            prs, rprs = pw[10], "prs"
            dve_ts(prs[:], pwr[:], ppc("sgn"), ALU.mult, reads=["pwr", "pp"], writes=[rprs])
            for k in range(9):
                dve_tt(Rt[:, :, k, :], Cc, bc16(prn[:, :, k]), ALU.mult, reads=["BCs", rprn], writes=[("R", k)])
                dve_tt(tmpB[:], Csw, bc16(pwi_[:, :, k]), ALU.mult, reads=["BCs", "pwi"], writes=["tmpB"], eng="pool")
                dve_tt(Rt[:, :, k, :], Rt[:, :, k, :], tmpB[:], ALU.subtract, reads=[("R", k), "tmpB"], writes=[("R", k)])
                dve_tt(R2[:, :, k, :], Csw, bc16(prs[:, :, k]), ALU.mult, reads=["BCs", rprs], writes=[("R2", k)])
                dve_tt(tmpA[:], Cc, bc16(pwi_[:, :, k]), ALU.mult, reads=["BCs", "pwi"], writes=["tmpA"], eng="pool")
                dve_tt(R2[:, :, k, :], R2[:, :, k, :], tmpA[:], ALU.subtract, reads=[("R2", k), "tmpA"], writes=[("R2", k)])
            RALL = [("R", k) for k in range(9)]
            R2ALL = [("R2", k) for k in range(9)]
            GALL = [("G", k) for k in range(8)]
            S.op("act", lambda e: e.activation(out=WY1_all[:].rearrange("p g (k c) -> p g k c", c=16), in_=Rt[:, :, 1:9, :], func=AF.Copy),
                 reads=RALL, writes=["WY1"])
            S.op("act", lambda e: e.activation(out=WY2_all[:].rearrange("p g (k c) -> p g k c", c=16), in_=R2[:, :, 1:9, :], func=AF.Copy),
                 reads=R2ALL, writes=["WY2"])
            for gb in range(8):
                pi_ = nps()

                def tr(e, gb=gb, pi_=pi_):
                    ins = None
                    for q in range(4):
                        g = gb * 4 + q
                        ins = e.transpose(ps[pi_][:, q * 128:(q + 1) * 128], Gt[:, g].rearrange("p a b -> p (a b)"), id32[:])
                    return ins
                S.op("pe", tr, reads=GALL + ["id32"], writes=[("ps", pi_)])
                pv = ps[pi_][:].rearrange("p (q m) -> p q m", m=128)
                act(WE_all[:, gb * 4:(gb + 1) * 4, :], pv, AF.Copy, reads=[("ps", pi_)], writes=[("WE", gb)])
                S.op("dve", lambda e, pv=pv, gb=gb: e.tensor_copy(out=WEs_all[:, gb * 4:(gb + 1) * 4, 0:64], in_=pv[:, :, 64:128]),
                     reads=[("ps", pi_), ("WE", gb)], writes=[("WEs", gb)])
                S.op("dve", lambda e, pv=pv, gb=gb: e.tensor_copy(out=WEs_all[:, gb * 4:(gb + 1) * 4, 64:128], in_=pv[:, :, 0:64]),
                     reads=[("ps", pi_), ("WE", gb)], writes=[("WEs", gb)])
            for gb in range(8):
                pi_ = nps()

                def kt(e, gb=gb, pi_=pi_):
                    ins = None
                    for q in range(4):
                        g = gb * 4 + q
                        ins = e.matmul(ps[pi_][0:16, q * 128:(q + 1) * 128], lhsT=Bb[:, g, :],
                                       rhs=Rt[:, g, 0:8, :].rearrange("p a b -> p (a b)"), start=True, stop=True)
                    return ins
                S.op("pe", kt, reads=RALL + ["Bb"], writes=[("ps", pi_)])
                pv = ps[pi_][0:16, :].rearrange("p (q m) -> p q m", m=128)
                act(KTs[0:16, gb * 4:(gb + 1) * 4, :], pv, AF.Copy, reads=[("ps", pi_)], writes=[("KT", gb)])
                for q in range(4):
                    g = gb * 4 + q
                    dve_stt(KTs[0:16, g, 0:16], id32[0:16, 0:16], dgs[0:16, g:g + 1], pv[:, q, 0:16], ALU.mult, ALU.add,
                            reads=["id32", "dgs", ("ps", pi_), ("KT", gb)], writes=[("KT", gb)])
            for ip in range(8):
                dma("sp", M_all[16 * ip:16 * ip + 16, :, 16 * ip:128], KTs[0:16, :, 0:(8 - ip) * 16],
                    reads=[("KT", gb) for gb in range(8)] + ["M_all"], writes=[("M", ip)])
            f_, rf = sc_new()
            dve_ts(f_, tu, 8.0, ALU.mult, reads=[rtu], writes=[rf])
            reduce_turns(f_, rf, sci[:])
            fh16 = carve(P0 + 97 * KB, [128, 32], BF16)
            S.op("dve", lambda e: e.tensor_copy(out=fh16, in_=f_), reads=[rf], writes=["fh16"])
            S.op("dve", lambda e: e.tensor_copy(out=ssc[:, 0, :], in_=fh16), reads=["fh16"], writes=["ssc0"])
            dve_tt(ssc[:, 1, :], f_, ssc[:, 0, :], ALU.subtract, reads=[rf, "ssc0"], writes=["ssc1"])
            S.op("dve", lambda e: e.tensor_copy(out=ssc[:, 2, :], in_=magk[:, :, 8]), reads=["magk"], writes=["ssc2"])
            for name, src, rr in (("WE", WE_all, [("WE", gb) for gb in range(8)]), ("M", M_all, [("M", ip) for ip in range(8)]),
                                  ("WY1", WY1_all, ["WY1"]), ("WY2", WY2_all, ["WY2"]), ("WEs", WEs_all, [("WEs", gb) for gb in range(8)])):
                if name in dbg_t:
                    dma("sp", dbg_t[name], src[:], reads=rr)
            if "ssc" in dbg_t:
                dma("sp", dbg_t["ssc"], ssc[:], reads=["ssc0", "ssc1", "ssc2"])
            S.barrier()

        UD = carve(40 * KB, [128, 4, 8, 512], BF16)
        YB = carve(72 * KB, [128, 4, L], BF16)
        WIN = carve(104 * KB, [128, 8, 2048], BF16)
        GLU = carve(104 * KB, [128, 4, 512], BF16)
        WOUT = carve(108 * KB, [128, 8, 1024], BF16)
        US = carve(136 * KB, [128, 32, 512], BF16)
        TB = 136 * KB

        if "a1" in phases:
            w_inv = w_in.rearrange("(kc p) m -> p kc m", p=128)
            for kc in range(8):
                dma("pool", WIN[:, kc, :], w_inv[:, kc, :], writes=[("WIN", kc)])
            WINR = [("WIN", kc) for kc in range(8)]
            xt = carve(TB, [128, 8, TT])
            hb = carve(TB + 16 * KB, [128, 8, TT], BF16)
            sq = [carve(TB + 24 * KB + i * KB, [128, TT], BF16) for i in range(2)]
            tmp = [carve(TB + 26 * KB + i * 2 * KB, [128, TT]) for i in range(2)]
            rstd = carve(TB + 30 * KB, [128, TT])
            srt = carve(TB + 32 * KB, [128, TT])
            cgb = [carve(TB + 34 * KB + i * 2 * KB, [128, TT]) for i in range(2)]
            cv = carve(TB + 38 * KB, [128, 4, TT + 2])
            acc = [carve(TB + 47 * KB + i * 2 * KB, [128, TT]) for i in range(2)]
            ybf = [carve(TB + 51 * KB + i * 2 * KB, [128, TT]) for i in range(2)]
            sqb = [carve(TB + 55 * KB + i * KB, [128, TT], BF16) for i in range(2)]
            rstdb = carve(TB + 57 * KB, [128, TT])
            srtb = carve(TB + 59 * KB, [128, TT])
            S.op("pool", lambda e: e.memset(cv[:, :, 0:2], 0.0), writes=["cvh"])
            xTv = xT.rearrange("(c p) t -> p c t", p=128)
            for t in range(ntiles):
                t0 = t * TT
                for c in range(8):
                    dma("sp", xt[:, c, :], xTv[:, c, t0:t0 + TT], writes=[("xt", c)])
                pst = nst()
                for c in range(8):
                    s_ = sq[c % 2]
                    act(s_, xt[:, c, :], AF.Square, reads=[("xt", c)], writes=[("sq", c % 2)])
                    S.op("pe", lambda e, s_=s_, c=c, pst=pst: e.matmul(ps[pst][:], lhsT=ones_b, rhs=s_, start=(c == 0), stop=(c == 7)),
                         reads=[("sq", c % 2), "cstb"], writes=[("ps", pst)])
                rstd_from_ps(pst, 1.0 / D, srt, rstd, "a1")
                for c in range(8):
                    tm = tmp[c % 2]
                    dve_tt(tm, xt[:, c, :], rstd, ALU.mult, reads=[("xt", c), "a1rstd"], writes=[("tmp", c % 2)])
                    act(hb[:, c, :], tm, AF.Identity, reads=[("tmp", c % 2)] + DER, writes=[("hb", c)],
                        scale=der[:, c:c + 1], bias=modT[:, c:c + 1])
                HB = [("hb", c) for c in range(8)]

                def proj(m):
                    pi_ = nps()
                    mmg(pi_, [(WIN[:, kc, m * 128:(m + 1) * 128], hb[:, kc, :]) for kc in range(8)], reads=WINR + HB)
                    return pi_
                for q in range(4):
                    pc_ = proj(8 + q)
                    cg_ = cgb[q % 2]
                    act(cg_, ps[pc_][:], AF.Copy, reads=[("ps", pc_)], writes=[("cg", q % 2)])
                    pv_ = proj(12 + q)
                    dve_tt(cv[:, q, 2:TT + 2], ps[pv_][:], cg_, ALU.mult, reads=[("ps", pv_), ("cg", q % 2)], writes=[("cv", q)])
                    ac = acc[q % 2]
                    cwc = lambda k, q=q: ppc("cw", k * 4 + q)
                    dve_ts(ac, cv[:, q, 0:TT], cwc(0), ALU.mult, reads=[("cv", q), "cvh", "pp"], writes=[("acc", q % 2)], eng="pool")
                    dve_stt(ac, cv[:, q, 1:TT + 1], cwc(1), ac, ALU.mult, ALU.add, reads=[("cv", q), "cvh", "pp", ("acc", q % 2)],
                            writes=[("acc", q % 2)], eng="pool")
                    dve_stt(ac, cv[:, q, 2:TT + 2], cwc(2), ac, ALU.mult, ALU.add, reads=[("cv", q), "pp", ("acc", q % 2)],
                            writes=[("acc", q % 2)], eng="pool")
                    S.op("pool", lambda e, q=q: e.tensor_copy(out=cv[:, q, 0:2], in_=cv[:, q, TT:TT + 2]),
                         reads=[("cv", q), ("acc", q % 2)], writes=["cvh", ("cv", q)])
                    pb_ = proj(4 + q)
                    yb_ = ybf[q % 2]
                    dve_tt(yb_, ps[pb_][:], ac, ALU.mult, reads=[("ps", pb_), ("acc", q % 2)], writes=[("ybf", q % 2)])
                    sb_ = sqb[q % 2]
                    act(sb_, yb_, AF.Square, reads=[("ybf", q % 2)], writes=[("sqb", q % 2)])
                    pn_ = nps()
                    mmg(pn_, [(bd64_b, sb_)], reads=[("sqb", q % 2), "cstb"])
                    rstd_from_ps(pn_, 1.0 / 64, srtb, rstdb, "a1b")
                    dve_stt(YB[:, q, t0:t0 + TT], yb_, ppc("goc", q), rstdb, ALU.mult, ALU.mult,
                            reads=[("ybf", q % 2), "a1brstd", "pp"], writes=[("YB", q, t)])
                for m in range(4):
                    pu_ = proj(m)
                    src = ps[pu_][:].rearrange("p (j i) -> p i j", i=8)
                    dst = UD[:, m, :, t * 64:(t + 1) * 64]
                    if m % 2 == 0:
                        act(dst, src, AF.Copy, reads=[("ps", pu_)], writes=[("UD", m, t)])
                    else:
                        S.op("dve", lambda e, dst=dst, src=src: e.tensor_copy(out=dst, in_=src), reads=[("ps", pu_)],
                             writes=[("UD", m, t)])
            if "UD" in dbg_t:
                dma("sp", dbg_t["UD"], UD[:].rearrange("p a b c -> p (a b c)"),
                    reads=[("UD", m, t) for m in range(4) for t in range(NT)])
            if "YB" in dbg_t:
                dma("sp", dbg_t["YB"], YB[:].rearrange("p a b -> p (a b)"),
                    reads=[("YB", q, t) for q in range(4) for t in range(NT)])
            S.barrier()

        if "ssm" in phases:
            USv = US[:].rearrange("p (m e) j -> p m e j", e=8)
            n_ = 0
            for i in range(8):
                for g8 in range(8):
                    dma("sp" if n_ % 2 == 0 else "act", USv[16 * i:16 * i + 16, :, g8, :], UD[16 * g8:16 * g8 + 16, :, i, :],
                        writes=[("US", i, g8)])
                    n_ += 1
            SB0 = 168 * KB
            angb = [carve(SB0 + i * 2 * KB, [128, 512]) for i in range(2)]
            angi = carve(SB0 + 4 * KB, [128, 512], I32)
            cosb = [carve(SB0 + 6 * KB + i * 2 * KB, [128, 512]) for i in range(2)]
            sinb = [carve(SB0 + 10 * KB + i * 2 * KB, [128, 512]) for i in range(2)]
            tA = carve(SB0 + 14 * KB, [128, 512])
            tB = carve(SB0 + 16 * KB, [128, 512])
            Ep = carve(SB0 + 18 * KB, [128, 512])
            zb = carve(SB0 + 20 * KB, [128, 512])
            P1 = [carve(SB0 + 22 * KB + i * 1028, [128, 513], BF16) for i in range(2)]
            P2 = [carve(SB0 + 22 * KB + 2056 + i * 1028, [128, 513], BF16) for i in range(2)]
            ab2 = carve(SB0 + 27 * KB, [128, 512])
            for i in range(2):
                S.op("pool", lambda e, i=i: e.memset(P1[i][:, 0:1], 0.0), writes=[("P1", i)])
                S.op("pool", lambda e, i=i: e.memset(P2[i][:, 0:1], 0.0), writes=[("P2", i)])
            for g in range(32):
                b = g % 2
                usg = US[:, g, :]
                USR = [("US", i, g % 8) for i in range(8)]
                q1 = nps()
                mmg(q1, [(WE_all[:, g, :], usg)], reads=USR)
                q2 = nps()
                mmg(q2, [(WEs_all[:, g, :], usg)], reads=USR)
                an = angb[b]
                dve_ts(an, iota[:], ssc[:, 0, g:g + 1], ALU.mult, reads=["iota"], writes=[("ang", b)])
                S.op("dve", lambda e, an=an: e.tensor_copy(out=angi, in_=an), reads=[("ang", b)], writes=["angi"])
                dve_tt(an, an, angi, ALU.subtract, reads=[("ang", b), "angi"], writes=[("ang", b)])
                dve_stt(an, iota[:], ssc[:, 1, g:g + 1], an, ALU.mult, ALU.add, reads=[("ang", b), "iota"], writes=[("ang", b)])
                S.op("dve", lambda e, an=an: e.tensor_copy(out=angi, in_=an), reads=[("ang", b)], writes=["angi"])
                dve_tt(an, an, angi, ALU.subtract, reads=[("ang", b), "angi"], writes=[("ang", b)])
                act(sinb[b], an, AF.Sin, reads=[("ang", b), "pp"], writes=[("sin", b)], scale=ppc("tpn"))
                act(ab2, an, AF.Abs, reads=[("ang", b)], writes=["ab2"])
                act(cosb[b], ab2, AF.Sin, reads=["ab2", "pp"], writes=[("cos", b)], scale=-2 * math.pi, bias=ppc("hpi"))
                dve_tt(tA, ps[q1][:], cosb[b], ALU.mult, reads=[("ps", q1), ("cos", b)], writes=["tA"])
                dve_tt(tB, ps[q2][:], sinb[b], ALU.mult, reads=[("ps", q2), ("sin", b)], writes=["tB"])
                dve_tt(Ep, tA, tB, ALU.add, reads=["tA", "tB"], writes=["Ep"], eng="pool")
                r8b = ssc[:, 2, g:g + 1].broadcast_to([128, 512])
                S.op("dve", lambda e, r8b=r8b: e.tensor_tensor_scan(out=zb, data0=r8b, data1=Ep, initial=0.0, op0=ALU.mult, op1=ALU.add),
                     reads=["Ep", "ssc"], writes=["zb"])
                dve_tt(P1[b][:, 1:513], zb, cosb[b], ALU.mult, reads=["zb", ("cos", b)], writes=[("P1", b)], eng="pool")
                dve_tt(P2[b][:, 1:513], zb, sinb[b], ALU.mult, reads=["zb", ("sin", b)], writes=[("P2", b)])
                py = nps()
                mmg(py, [(M_all[:, g, :], usg), (WY1_all[:, g, :], P1[b][:, 0:512]), (WY2_all[:, g, :], P2[b][:, 0:512])],
                    reads=USR + [("P1", b), ("P2", b)])
                S.op("act", lambda e, py=py, usg=usg: e.activation(out=usg, in_=ps[py][:], func=AF.Gelu_apprx_tanh),
                     reads=[("ps", py)] + USR, writes=[("ZS", g)])
            if "ZS" in dbg_t:
                dma("sp", dbg_t["ZS"], US[:].rearrange("p a b -> p (a b)"), reads=[("ZS", g) for g in range(32)])
            n_ = 0
            for i in range(8):
                for g8 in range(8):
                    dma("sp" if n_ % 2 == 0 else "act", UD[16 * g8:16 * g8 + 16, :, i, :], USv[16 * i:16 * i + 16, :, g8, :],
                        reads=[("ZS", m * 8 + g8) for m in range(4)], writes=[("ZF", i, g8)])
                    n_ += 1
            S.barrier()

        if "a2" in phases:
            glv = glu_w.rearrange("(kc p) m -> p kc m", p=128)
            wov = w_out.rearrange("(kc p) m -> p kc m", p=128)
            for kc in range(4):
                dma("pool", GLU[:, kc, :], glv[:, kc, :], writes=[("GLU", kc)])
            for kc in range(8):
                dma("pool", WOUT[:, kc, :], wov[:, kc, :], writes=[("WOUT", kc)])
            TB2 = 124 * KB
            xt = carve(TB2, [128, 8, TT])
            zt = carve(TB2 + 16 * KB, [128, 4, TT], BF16)
            sig = [carve(TB2 + 20 * KB + i * 2 * KB, [128, TT]) for i in range(2)]
            ya = [carve(TB2 + 24 * KB + i * 2 * KB, [128, TT]) for i in range(2)]
            sqa = [carve(TB2 + 28 * KB + i * KB, [128, TT], BF16) for i in range(2)]
            yan = carve(TB2 + 30 * KB, [128, 4, TT], BF16)
            rstd = carve(TB2 + 34 * KB, [128, TT])
            srt = carve(TB2 + 36 * KB, [128, TT])
            ysb = carve(TB2 + 38 * KB, [128, 8, TT])
            sq = [carve(TB2 + 54 * KB + i * KB, [128, TT], BF16) for i in range(2)]
            tmp = [carve(TB2 + 56 * KB + i * 2 * KB, [128, TT]) for i in range(2)]
            rstd2 = carve(TB2 + 60 * KB, [128, TT])
            srt2 = carve(TB2 + 62 * KB, [128, TT])
            xTv = xT.rearrange("(c p) t -> p c t", p=128)
            x1v = x1s.rearrange("(c p) t -> p c t", p=128)
            for t in range(ntiles):
                t0 = t * TT
                for c in range(8):
                    dma("sp", xt[:, c, :], xTv[:, c, t0:t0 + TT], writes=[("xt", c)])
                for m in range(4):
                    src = UD[:, m, :, t * 64:(t + 1) * 64].rearrange("p i j -> p j i")
                    dst = zt[:, m, :].rearrange("p (j i) -> p j i", i=8)
                    S.op("pool", lambda e, src=src, dst=dst: e.tensor_copy(out=dst, in_=src),
                         reads=[("ZF", i, g8) for i in range(8) for g8 in range(8)], writes=[("zt", m)])
                ZT = [("zt", m) for m in range(4)]
                for m in range(4):
                    pg = nps()
                    mmg(pg, [(GLU[:, kc, m * 128:(m + 1) * 128], zt[:, kc, :]) for kc in range(4)],
                        reads=ZT + [("GLU", kc) for kc in range(4)])
                    act(sig[m % 2], ps[pg][:], AF.Sigmoid, reads=[("ps", pg), "pp"], writes=[("sig", m % 2)], bias=ppc("glub", m))
                    dve_tt(ya[m % 2], zt[:, m, :], sig[m % 2], ALU.mult, reads=[("zt", m), ("sig", m % 2)], writes=[("ya", m % 2)])
                    act(sqa[m % 2], ya[m % 2], AF.Square, reads=[("ya", m % 2)], writes=[("sqa", m % 2)])
                    pn_ = nps()
                    mmg(pn_, [(bd16_b, sqa[m % 2])], reads=[("sqa", m % 2), "cstb"])
                    rstd_from_ps(pn_, 1.0 / 16, srt, rstd, "a2h")
                    dve_stt(yan[:, m, :], ya[m % 2], ppc("gos", m), rstd, ALU.mult, ALU.mult,
                            reads=[("ya", m % 2), "a2hrstd", "pp"], writes=[("yan", m)])
                YAN = [("yan", m) for m in range(4)]
                pst = nst()
                for mo in range(8):
                    po = nps()
                    pairs = [(WOUT[:, kc, mo * 128:(mo + 1) * 128], yan[:, kc, :]) for kc in range(4)]
                    pairs += [(WOUT[:, 4 + kc, mo * 128:(mo + 1) * 128], YB[:, kc, t0:t0 + TT]) for kc in range(4)]
                    mmg(po, pairs, reads=YAN + [("WOUT", kc) for kc in range(8)])
                    S.op("dve", lambda e, mo=mo, po=po: e.tensor_copy(out=ysb[:, mo, :], in_=ps[po][:]), reads=[("ps", po)],
                         writes=[("ysb", mo)])
                    act(sq[mo % 2], ysb[:, mo, :], AF.Square, reads=[("ysb", mo)], writes=[("sq", mo % 2)])
                    S.op("pe", lambda e, mo=mo, pst=pst: e.matmul(ps[pst][:], lhsT=ones_b, rhs=sq[mo % 2], start=(mo == 0), stop=(mo == 7)),
                         reads=[("sq", mo % 2), "cstb"], writes=[("ps", pst)])
                rstd_from_ps(pst, 1.0 / D, srt2, rstd2, "a2p")
                for c in range(8):
                    tm = tmp[c % 2]
                    dve_tt(tm, ysb[:, c, :], rstd2, ALU.mult, reads=[("ysb", c), "a2prstd"], writes=[("tmp", c % 2)], eng="pool")
                    dve_stt(xt[:, c, :], tm, der[:, 8 + c:9 + c], xt[:, c, :], ALU.mult, ALU.add,
                            reads=[("tmp", c % 2), ("xt", c)] + DER, writes=[("xt", c)])
                    dma("sp", x1v[:, c, t0:t0 + TT], xt[:, c, :], reads=[("xt", c)], writes=[("x1s", t, c)])
            S.barrier()

        if "ffn" in phases:
            WUP = carve(0, [128, 8, 2 * DFF], BF16)
            WDN = carve(88 * KB, [128, NF, D], BF16)
            wuv = w_up.rearrange("(kc p) m -> p kc m", p=128)
            wdv = w_down.rearrange("(kc p) m -> p kc m", p=128)
            for kc in range(8):
                for hh in range(2):
                    for q4 in range(2):
                        c0 = hh * DFF + q4 * (DFF // 2)
                        dma("pool", WUP[:, kc, c0:c0 + DFF // 2], wuv[:, kc, c0:c0 + DFF // 2], writes=[("WUP", kc, hh, q4)])
            for kc in range(NF):
                dma("pool", WDN[:, kc, :], wdv[:, kc, :], writes=[("WDN", kc)])
            WUPR = [("WUP", kc, hh, q4) for kc in range(8) for hh in range(2) for q4 in range(2)]
            WDNR = [("WDN", kc) for kc in range(NF)]
            TB3 = 132 * KB
            xt = carve(TB3, [128, 8, TT])
            h2 = carve(TB3 + 16 * KB, [128, 8, TT], BF16)
            actb = carve(TB3 + 24 * KB, [128, NF, TT], BF16)
            sq = [carve(TB3 + 46 * KB + i * KB, [128, TT], BF16) for i in range(2)]
            tmp = [carve(TB3 + 48 * KB + i * 2 * KB, [128, TT]) for i in range(2)]
            rstd = carve(TB3 + 52 * KB, [128, TT])
            srt = None
            upb = [carve(TB3 + 54 * KB + i * 2064, [128, TT + 2]) for i in range(2)]
            acc = [carve(TB3 + 54 * KB + 4128 + i * 2 * KB, [128, TT]) for i in range(2)]
            xr = [carve(TB3 + 54 * KB + 4128 + 4 * KB + i * 2 * KB, [128, TT]) for i in range(2)]
            x1v = x1s.rearrange("(c p) t -> p c t", p=128)
            oTv = outT.rearrange("(c p) t -> p c t", p=128)
            src_v = x1v if "a2" in phases else xT.rearrange("(c p) t -> p c t", p=128)
            for t in range(ntiles):
                t0 = t * TT
                for c in range(8):
                    dma("sp", xt[:, c, :], src_v[:, c, t0:t0 + TT], reads=[("x1s", t, c)], writes=[("xt", c)])
                pst = nst()
                for c in range(8):
                    s_ = sq[c % 2]
                    act(s_, xt[:, c, :], AF.Square, reads=[("xt", c)], writes=[("sq", c % 2)])
                    S.op("pe", lambda e, s_=s_, c=c, pst=pst: e.matmul(ps[pst][:], lhsT=ones_b, rhs=s_, start=(c == 0), stop=(c == 7)),
                         reads=[("sq", c % 2), "cstb"], writes=[("ps", pst)])
                rstd_from_ps(pst, 1.0 / D, srt, rstd, "b")
                for c in range(8):
                    tm = tmp[c % 2]
                    dve_tt(tm, xt[:, c, :], rstd, ALU.mult, reads=[("xt", c), "brstd"], writes=[("tmp", c % 2)])
                    act(h2[:, c, :], tm, AF.Identity, reads=[("tmp", c % 2)] + DER, writes=[("h2", c)],
                        scale=der[:, 16 + c:17 + c], bias=modT[:, 24 + c:25 + c])
                H2 = [("h2", c) for c in range(8)]
                if upto < 2:
                    continue
                for f in range(NF if upto >= 3 else 1):
                    hid = []
                    for which in range(2):
                        ch = which * NF + f
                        pu = nps()
                        mmg(pu, [(WUP[:, kc, ch * 128:(ch + 1) * 128], h2[:, kc, :]) for kc in range(8)], reads=WUPR + H2)
                        ub = upb[which]
                        S.op("pool", lambda e, ub=ub, ch=ch: e.tensor_copy(out=ub[:, 0:2], in_=halo[:, ch, :]),
                             reads=["halo", ("acc", which)], writes=[("uph", which)])
                        act(ub[:, 2:TT + 2], ps[pu][:], AF.Copy, reads=[("ps", pu), ("acc", which)], writes=[("upb", which)])
                        S.op("pool", lambda e, ub=ub, ch=ch: e.tensor_copy(out=halo[:, ch, :], in_=ub[:, TT:TT + 2]),
                             reads=[("upb", which)], writes=["halo"])
                        ac = acc[which]
                        fw = lambda k, ch=ch: ppc("fcw", k * 44 + ch)
                        eng = "dve" if which == 0 else "pool"
                        dve_ts(ac, ub[:, 0:TT], fw(0), ALU.mult, reads=[("upb", which), ("uph", which), "pp"], writes=[("acc", which)], eng=eng)
                        dve_stt(ac, ub[:, 1:TT + 1], fw(1), ac, ALU.mult, ALU.add, reads=[("upb", which), ("uph", which), "pp", ("acc", which)],
                                writes=[("acc", which)], eng=eng)
                        dve_stt(ac, ub[:, 2:TT + 2], fw(2), ac, ALU.mult, ALU.add, reads=[("upb", which), "pp", ("acc", which)],
                                writes=[("acc", which)], eng=eng)
                        hid.append(ac)
                    act(hid[0], hid[0], AF.Silu, reads=[("acc", 0)], writes=[("acc", 0)])
                    dve_tt(actb[:, f, :], hid[0], hid[1], ALU.mult, reads=[("acc", 0), ("acc", 1)], writes=[("actb", f)])
                ACTB = [("actb", f) for f in range(NF)]
                if upto < 4:
                    continue
                pst = nst()
                for mo in range(8):
                    po = nps()
                    mmg(po, [(WDN[:, f, mo * 128:(mo + 1) * 128], actb[:, f, :]) for f in range(NF)], reads=ACTB + WDNR)
                    S.op("dve", lambda e, mo=mo, po=po: e.tensor_copy(out=xt[:, mo, :], in_=ps[po][:]), reads=[("ps", po)] + H2,
                         writes=[("xt", mo)])
                    act(sq[mo % 2], xt[:, mo, :], AF.Square, reads=[("xt", mo)], writes=[("sq", mo % 2)])
                    S.op("pe", lambda e, mo=mo, pst=pst: e.matmul(ps[pst][:], lhsT=ones_b, rhs=sq[mo % 2], start=(mo == 0), stop=(mo == 7)),
                         reads=[("sq", mo % 2), "cstb"], writes=[("ps", pst)])
                rstd_from_ps(pst, 1.0 / D, srt, rstd, "b")
                if upto < 5:
                    continue
                for c in range(8):
                    tm = tmp[c % 2]
                    xr_ = xr[c % 2]
                    dma("sp", xr_, src_v[:, c, t0:t0 + TT], reads=[("x1s", t, c)], writes=[("xr", c % 2)])
                    dve_tt(tm, xt[:, c, :], rstd, ALU.mult, reads=[("xt", c), "brstd"], writes=[("tmp", c % 2)], eng="pool")
                    dve_stt(xr_, tm, der[:, 24 + c:25 + c], xr_, ALU.mult, ALU.add,
                            reads=[("tmp", c % 2), ("xr", c % 2)] + DER, writes=[("xr", c % 2)])
                    dma("sp", oTv[:, c, t0:t0 + TT], xr_, reads=[("xr", c % 2)], writes=[("out", t, c)])
        S.barrier()
        S.emit()
    return nc


_CACHE = {}


def kernel(**inp):
    inp = {k: np.asarray(v) for k, v in inp.items()}
    B = inp["x"].shape[0]
    cst, iota = make_consts()
    sp1, BC, dg = pack_ssm(inp)
    shared = {
        "w_ada": np.ascontiguousarray(inp["w_ada"][0], dtype=np.float32),
        "w_in": np.ascontiguousarray(inp["w_in"][0], dtype=np.float32),
        "glu_w": np.ascontiguousarray(inp["glu_w"][0], dtype=np.float32),
        "w_out": np.ascontiguousarray(inp["w_out"][0], dtype=np.float32),
        "w_up": np.ascontiguousarray(inp["w_up"][0], dtype=np.float32),
        "w_down": np.ascontiguousarray(inp["w_down"][0], dtype=np.float32),
        "sp1": sp1, "BC": BC, "dg": dg, "cst": cst, "iota": iota,
    }
    in_maps = []
    for b in range(B):
        m = dict(shared)
        m["xT"] = np.ascontiguousarray(inp["x"][b].T, dtype=np.float32)
        m["pp"] = pack_pp(inp, b)
        in_maps.append(m)
    if "nc" not in _CACHE:
        _CACHE["nc"] = build_nc()
    nc = _CACHE["nc"]
    res = run_bass_kernel_spmd(nc, in_maps, core_ids=list(range(B)))
    out = np.stack([np.asarray(res.results[b]["outT"]).T for b in range(B)], axis=0)
    return np.ascontiguousarray(out, dtype=np.float32)
```

```python
import contextlib
import math
import numpy as np
import concourse.bass as bass
import concourse.mybir as mybir
from concourse.bass_utils import run_bass_kernel_spmd

F32 = mybir.dt.float32
BF16 = mybir.dt.bfloat16
I32 = mybir.dt.int32
ALU = mybir.AluOpType
AF = mybir.ActivationFunctionType

NDSEM = 28
NSWSEM = 8
L = 4096
D = 1024
TT = 512
NT = L // TT
DFF = 2816
NF = DFF // 128
EPS = 1e-6

DEBUG = False


class Sched:
    ENG = ("pe", "act", "dve", "pool", "sp")

    def __init__(self, nc):
        self.nc = nc
        self.q = {e: [] for e in self.ENG}
        self.cnt = {e: 0 for e in self.ENG}
        self.waited = {e: {} for e in self.ENG}
        self.res = {}
        self.ndma = 0
        self.ndma_sw = 0
        self.dma_last = [None] * NDSEM

    def _deps(self, reads, writes):
        deps = []
        for r in reads:
            st = self.res.get(r)
            if st and st["w"] is not None:
                deps.append(st["w"])
        for w in writes:
            st = self.res.get(w)
            if st:
                if st["w"] is not None:
                    deps.append(st["w"])
                deps.extend(st["r"])
        return deps

    def op(self, eng, fn, reads=(), writes=(), dma=False):
        deps = self._deps(reads, writes)
        if dma:
            if eng == "pool":
                k = self.ndma_sw % NSWSEM
                j = NDSEM - NSWSEM + k
                val = 16 * (self.ndma_sw // NSWSEM + 1)
                self.ndma_sw += 1
            else:
                j = self.ndma % (NDSEM - NSWSEM)
                val = 16 * (self.ndma // (NDSEM - NSWSEM) + 1)
                self.ndma += 1
            if self.dma_last[j] is not None:
                deps.append(self.dma_last[j])
            tok = ("d%d" % j, val)
            self.dma_last[j] = tok
        else:
            self.cnt[eng] += 1
            tok = (eng, self.cnt[eng])
        wd = self.waited[eng]
        m = {}
        for (s, v) in deps:
            if s == eng and eng == "pe":
                continue
            if wd.get(s, 0) >= v:
                continue
            m[s] = max(m.get(s, 0), v)
        for s, v in m.items():
            wd[s] = v
        self.q[eng].append((list(m.items()), fn, tok, dma))
        for r in reads:
            st = self.res.setdefault(r, {"w": None, "r": []})
            st["r"].append(tok)
            if len(st["r"]) > 64:
                st["r"] = st["r"][-64:] if False else st["r"]
        for w in writes:
            self.res[w] = {"w": tok, "r": []}
        return tok

    def barrier(self):
        toks = [(e, self.cnt[e]) for e in self.ENG if self.cnt[e] > 0]
        toks += [t for t in self.dma_last if t is not None]
        for e in self.ENG:
            m = {}
            wd = self.waited[e]
            for s, v in toks:
                if wd.get(s, 0) >= v:
                    continue
                m[s] = v
                wd[s] = v
            if m:
                self.q[e].append((list(m.items()), None, None, False))
        self.res = {}

    def emit(self):
        nc = self.nc
        with contextlib.ExitStack() as es:
            sems = {}
            for e in self.ENG:
                sems[e] = es.enter_context(nc.semaphore("s_" + e))
            for j in range(NDSEM):
                sems["d%d" % j] = es.enter_context(nc.semaphore("s_d%d" % j))
            block = es.enter_context(nc.Block())

            def run(engname, eh):
                for waits, fn, tok, dma in self.q[engname]:
                    for s, v in waits:
                        eh.wait_ge(sems[s], v)
                    if fn is None:
                        continue
                    ins = fn(eh)
                    ins.then_inc(sems[tok[0]], 16 if dma else 1)

            @block.tensor
            def _(e):
                run("pe", e)

            @block.scalar
            def _(e):
                run("act", e)

            @block.vector
            def _(e):
                run("dve", e)

            @block.gpsimd
            def _(e):
                run("pool", e)

            @block.sync
            def _(e):
                run("sp", e)


PP_FIELDS = [("gpm", 8), ("gqm", 8), ("gpf", 8), ("gqf", 8), ("bada", 48), ("ssd", 4), ("glub", 4),
             ("gos", 4), ("goc", 4), ("cw", 12), ("fcw", 132), ("c", 8), ("sgn", 1), ("nsg", 1),
             ("eps", 1), ("hpi", 1), ("kv", 9), ("tpn", 1), ("zero", 1)]
PP_OFF = {}
_o = 0
for _n, _w in PP_FIELDS:
    PP_OFF[_n] = (_o, _w)
    _o += _w
NPP = _o


def _colmajor(v, n):
    return np.ascontiguousarray(np.asarray(v, np.float32).reshape(n, 128).T)


def pack_pp(inp, b):
    pp = np.zeros((128, NPP), np.float32)

    def put(name, arr):
        o, w = PP_OFF[name]
        pp[:, o:o + w] = np.asarray(arr, np.float32).reshape(128, w)

    put("gpm", _colmajor(inp["g_pre_mix"][0], 8))
    put("gqm", _colmajor(inp["g_post_mix"][0], 8))
    put("gpf", _colmajor(inp["g_pre_ffn"][0], 8))
    put("gqf", _colmajor(inp["g_post_ffn"][0], 8))
    put("bada", _colmajor(inp["b_ada"][0], 48))
    put("ssd", _colmajor(inp["ssm_d"][0], 4))
    put("glub", _colmajor(inp["glu_b"][0], 4))
    put("gos", _colmajor(inp["g_out_ssm"][0], 4))
    put("goc", _colmajor(inp["g_out_conv"][0], 4))
    cw = np.stack([_colmajor(inp["conv_w"][0][k], 4) for k in range(3)], axis=1)
    put("cw", cw)
    fcw = np.stack([_colmajor(inp["ffn_conv_w"][0][k], 44) for k in range(3)], axis=1)
    put("fcw", fcw)
    put("c", _colmajor(inp["c"][b], 8))
    sgn = np.concatenate([-np.ones(64), np.ones(64)]).astype(np.float32)
    put("sgn", sgn)
    put("nsg", -sgn)
    put("eps", np.full(128, EPS, np.float32))
    put("hpi", np.full(128, math.pi / 2, np.float32))
    put("kv", np.tile(np.arange(9, dtype=np.float32), (128, 1)))
    put("tpn", (-sgn) * np.float32(2 * math.pi))
    return pp


def pack_ssm(inp):
    lre = np.asarray(inp["ssm_lam_re"][0], np.float32).T
    lim = np.asarray(inp["ssm_lam_im"][0], np.float32).T
    ls = np.tile(np.asarray(inp["ssm_log_step"][0], np.float32)[None, :], (64, 1))
    sp1 = np.stack([np.concatenate([a, a], 0) for a in (lre, lim, ls)], axis=1)
    bre = np.transpose(np.asarray(inp["ssm_b_re"][0], np.float32), (1, 0, 2))
    bim = np.transpose(np.asarray(inp["ssm_b_im"][0], np.float32), (1, 0, 2))
    cre = np.transpose(np.asarray(inp["ssm_c_re"][0], np.float32), (2, 0, 1))
    cim = np.transpose(np.asarray(inp["ssm_c_im"][0], np.float32), (2, 0, 1))
    BC = np.stack([np.concatenate([bre, bim], 0), np.concatenate([bim, bre], 0),
                   np.concatenate([cre, cim], 0), np.concatenate([cim, cre], 0)], axis=1)
    dg = np.ascontiguousarray(np.asarray(inp["ssm_d"][0], np.float32).reshape(32, 16).T)
    return (np.ascontiguousarray(sp1, dtype=np.float32), np.ascontiguousarray(BC, dtype=np.float32), dg)


def make_consts():
    ident = np.eye(128, dtype=np.float32)
    ones = np.ones((128, 128), np.float32)
    bd16 = np.kron(np.eye(8, dtype=np.float32), np.ones((16, 16), np.float32))
    bd64 = np.kron(np.eye(2, dtype=np.float32), np.ones((64, 64), np.float32))
    cst = np.stack([ident, ones, bd16, bd64], axis=1)
    iota = np.tile(np.arange(512, dtype=np.float32)[None, :], (128, 1))
    return np.ascontiguousarray(cst), iota


def build_nc(phases=("prep", "a1", "ssm", "a2", "ffn"), dbg=(), ntiles=NT, upto=99):
    nc = bass.Bass("TRN2", target_bir_lowering=False)
    dt_ = lambda name, shape, kind="ExternalInput", dt=F32: nc.dram_tensor(name, list(shape), dt, kind=kind).ap()
    xT = dt_("xT", [D, L])
    outT = dt_("outT", [D, L], "ExternalOutput")
    x1s = outT
    w_ada = dt_("w_ada", [D, 6 * D])
    w_in = dt_("w_in", [D, 2048])
    glu_w = dt_("glu_w", [512, 512])
    w_out = dt_("w_out", [D, D])
    w_up = dt_("w_up", [D, 2 * DFF])
    w_down = dt_("w_down", [DFF, D])
    pp_d = dt_("pp", [128, NPP])
    sp1_d = dt_("sp1", [128, 3, 32])
    BC_d = dt_("BC", [128, 4, 32, 16])
    dg_d = dt_("dg", [16, 32])
    cst_d = dt_("cst", [128, 4, 128])
    iota_d = dt_("iota", [128, 512])
    dbg_t = {}
    for name, shape, ddt in dbg:
        dbg_t[name] = dt_("dbg_" + name, shape, "ExternalOutput", dt=ddt)

    es = contextlib.ExitStack()
    with es:
        ARENA_B = 200 * 1024
        arena = es.enter_context(nc.sbuf_tensor("arena", [128, ARENA_B // 4], F32))
        sb = lambda name, shape, dt=F32: es.enter_context(nc.sbuf_tensor("s_" + name, list(shape), dt))
        cstb = sb("cstb", [128, 4, 128], BF16)
        id32 = sb("id32", [128, 128], F32)
        iota = sb("iota", [128, 512], F32)
        pp = sb("pp", [128, NPP], F32)
        modT = sb("modT", [128, 48], F32)
        der = sb("der", [128, 32], F32)
        cs = sb("cs", [128, 8], F32)
        ssc = sb("ssc", [128, 4, 32], F32)
        halo = sb("halo", [128, 44, 2], F32)
        dgs = sb("dgs", [16, 32], F32)
        ps = [es.enter_context(nc.psum_tensor("ps%d" % i, [128, 512], F32)) for i in range(8)]

        def carve(off, shape, dt=F32):
            esz = 4 if dt in (F32, I32) else 2
            n = 1
            for s in shape[1:]:
                n *= s
            nb = n * esz
            assert off % 4 == 0 and off + nb <= ARENA_B, (off, shape, nb)
            w0, w1 = off // 4, (off + nb + 3) // 4
            ap = arena[:, w0:w1]
            if dt != F32:
                ap = ap.bitcast(dt)
                if esz == 2 and (nb % 4):
                    ap = ap[:, 0:n]
            if len(shape) == 3:
                ap = ap.rearrange("p (a b) -> p a b", b=shape[2])
            elif len(shape) == 4:
                ap = ap.rearrange("p (a b c) -> p a b c", b=shape[2], c=shape[3])
            return ap

        KB = 1024
        S = Sched(nc)
        ppc = lambda name, i=0, n=1: pp[:, PP_OFF[name][0] + i:PP_OFF[name][0] + i + n]
        ident_b = cstb[:, 0, :]
        ones_b = cstb[:, 1, :]
        bd16_b = cstb[:, 2, :]
        bd64_b = cstb[:, 3, :]

        psrr = [0]

        def nps():
            i = psrr[0] % 6
            psrr[0] += 1
            return i

        strr = [0]

        def nst():
            i = 6 + strr[0] % 2
            strr[0] += 1
            return i

        def dma(eng, out, in_, reads=(), writes=()):
            return S.op(eng, lambda e: e.dma_start(out=out, in_=in_), reads=reads, writes=writes, dma=True)

        def mmg(psi, pairs, reads, cols=None, writes=None):
            out = ps[psi][:] if cols is None else cols

            def fn(e):
                n = len(pairs)
                ins = None
                for i, (l, r) in enumerate(pairs):
                    ins = e.matmul(out, lhsT=l, rhs=r, start=(i == 0), stop=(i == n - 1))
                return ins
            return S.op("pe", fn, reads=reads, writes=[("ps", psi)] if writes is None else writes)

        dma("sp", pp[:], pp_d, writes=["pp"])
        dma("pool", cstb[:], cst_d, writes=["cstb"])
        dma("sp", id32[:], cst_d[:, 0, :], writes=["id32"])
        dma("sp", iota[:], iota_d, writes=["iota"])
        dma("sp", dgs[:], dg_d, writes=["dgs"])
        S.op("pool", lambda e: e.memset(halo[:], 0.0), writes=["halo"])

        if "prep" in phases:
            S.op("act", lambda e: e.activation(out=cs[:], in_=ppc("c", 0, 8), func=AF.Silu),
                 reads=["pp"], writes=["cs"])
            wa = [carve(40 * KB + i * 24 * KB, [128, 8, 768]) for i in range(2)]
            wav = w_ada.rearrange("(kc p) m -> p kc m", p=128)
            pm = nst()
            for pc in range(8):
                buf = wa[pc % 2]
                for kc in range(8):
                    dma("sp", buf[:, kc, :], wav[:, kc, pc * 768:(pc + 1) * 768], writes=[("wa", pc % 2, kc)])
                for mb in range(6):
                    m = pc * 6 + mb
                    mmg(pm, [(buf[:, kc, mb * 128:(mb + 1) * 128], cs[:, kc:kc + 1]) for kc in range(8)],
                        reads=[("wa", pc % 2, kc) for kc in range(8)] + ["cs"], cols=ps[pm][:, m:m + 1])
            S.op("dve", lambda e: e.tensor_tensor(out=modT[:], in0=ps[pm][:, 0:48], in1=ppc("bada", 0, 48), op=ALU.add),
                 reads=[("ps", pm), "pp"], writes=["modT"])
            S.op("dve", lambda e: e.tensor_scalar(out=der[:, 0:8], in0=modT[:, 8:16], scalar1=1.0, scalar2=None, op0=ALU.add),
                 reads=["modT"], writes=["der0"])
            S.op("dve", lambda e: e.tensor_tensor(out=der[:, 0:8], in0=der[:, 0:8], in1=ppc("gpm", 0, 8), op=ALU.mult),
                 reads=["der0", "pp"], writes=["der0"])
            S.op("dve", lambda e: e.tensor_tensor(out=der[:, 8:16], in0=modT[:, 16:24], in1=ppc("gqm", 0, 8), op=ALU.mult),
                 reads=["modT", "pp"], writes=["der1"])
            S.op("dve", lambda e: e.tensor_scalar(out=der[:, 16:24], in0=modT[:, 32:40], scalar1=1.0, scalar2=None, op0=ALU.add),
                 reads=["modT"], writes=["der2"])
            S.op("dve", lambda e: e.tensor_tensor(out=der[:, 16:24], in0=der[:, 16:24], in1=ppc("gpf", 0, 8), op=ALU.mult),
                 reads=["der2", "pp"], writes=["der2"])
            S.op("dve", lambda e: e.tensor_tensor(out=der[:, 24:32], in0=modT[:, 40:48], in1=ppc("gqf", 0, 8), op=ALU.mult),
                 reads=["modT", "pp"], writes=["der3"])
            if "modT" in dbg_t:
                dma("sp", dbg_t["modT"], modT[:], reads=["modT"])
        DER = ["der0", "der1", "der2", "der3", "modT", "pp"]

        def dve_tt(out, a, b, op, reads, writes, eng="dve"):
            return S.op(eng, lambda e: e.tensor_tensor(out=out, in0=a, in1=b, op=op), reads=reads, writes=writes)

        def dve_ts(out, a, s1, op0, reads, writes, s2=None, op1=None, eng="dve"):
            if op1 is None:
                return S.op(eng, lambda e: e.tensor_scalar(out=out, in0=a, scalar1=s1, scalar2=None, op0=op0),
                            reads=reads, writes=writes)
            return S.op(eng, lambda e: e.tensor_scalar(out=out, in0=a, scalar1=s1, scalar2=s2, op0=op0, op1=op1),
                        reads=reads, writes=writes)

        def dve_stt(out, a, s, b, op0, op1, reads, writes, eng="dve"):
            return S.op("dve", lambda e: e.scalar_tensor_tensor(out=out, in0=a, scalar=s, in1=b, op0=op0, op1=op1),
                        reads=reads, writes=writes)

        def act(out, in_, func, reads, writes, scale=1.0, bias=None):
            if bias is None:
                return S.op("act", lambda e: e.activation(out=out, in_=in_, func=func, scale=scale),
                            reads=reads, writes=writes)
            return S.op("act", lambda e: e.activation(out=out, in_=in_, func=func, scale=scale, bias=bias),
                        reads=reads, writes=writes)

        def rstd_from_ps(psi, inv_n, srt, rstd, tag):
            act(rstd, ps[psi][:], AF.Sqrt, reads=[("ps", psi), "pp"], writes=[tag + "rstd"], scale=inv_n, bias=ppc("eps"))
            S.op("dve", lambda e: e.reciprocal(out=rstd, in_=rstd), reads=[tag + "rstd"], writes=[tag + "rstd"])

        MATS = 0
        M_all = carve(MATS + 0 * KB, [128, 32, 128], BF16)
        WE_all = carve(MATS + 8 * KB, [128, 32, 128], BF16)
        WEs_all = carve(MATS + 16 * KB, [128, 32, 128], BF16)
        WY1_all = carve(MATS + 24 * KB, [128, 32, 128], BF16)
        WY2_all = carve(MATS + 32 * KB, [128, 32, 128], BF16)
        if "prep" in phases:
            P0 = 100 * KB
            BCs = carve(P0, [128, 4, 32, 16])
            sp1 = carve(P0 + 8 * KB, [128, 3, 32])
            sc_ = [carve(P0 + 9 * KB + i * 128, [128, 32]) for i in range(24)]
            pw = [carve(P0 + 12 * KB + i * 1152, [128, 32, 9]) for i in range(11)]
            Bb = carve(P0 + 26 * KB, [128, 32, 16])
            Bbs = carve(P0 + 28 * KB, [128, 32, 16])
            Gt = carve(P0 + 30 * KB, [128, 32, 8, 16])
            Rt = carve(P0 + 46 * KB, [128, 32, 9, 16])
            R2 = carve(P0 + 64 * KB, [128, 32, 9, 16])
            tmpA = carve(P0 + 82 * KB, [128, 32, 16])
            tmpB = carve(P0 + 84 * KB, [128, 32, 16])
            KTs = carve(P0 + 86 * KB, [128, 32, 128], BF16)
            pwi = carve(P0 + 94 * KB, [128, 32, 9], I32)
            sci = carve(P0 + 96 * KB, [128, 32], I32)
            dma("sp", BCs[:], BC_d, writes=["BCs"])
            dma("sp", sp1[:], sp1_d, writes=["sp1"])
            S.op("pool", lambda e: e.memset(M_all[:], 0.0), writes=["M_all"])
            k_ = [0]

            def sc_new():
                k_[0] += 1
                return sc_[k_[0] - 1], ("sc", k_[0] - 1)

            def tt(a, b, op, ra, rb, eng="dve"):
                o, ro = sc_new()
                dve_tt(o, a, b, op, reads=[ra, rb], writes=[ro], eng=eng)
                return o, ro

            lre, lim, lst = sp1[:, 0, :], sp1[:, 1, :], sp1[:, 2, :]
            re_, rre = sc_new()
            dve_ts(re_, lre, -1e-4, ALU.min, reads=["sp1"], writes=[rre])
            dtt, rdt = sc_new()
            act(dtt, lst, AF.Exp, reads=["sp1"], writes=[rdt])
            a_, ra = tt(re_, dtt, ALU.mult, rre, rdt)
            th, rth = tt(lim, dtt, ALU.mult, "sp1", rdt)
            t_, rt = sc_new()
            dve_ts(t_, a_, 1.0 / 6, ALU.mult, reads=[ra], writes=[rt], s2=1.0, op1=ALU.add)
            for kk in (5, 4, 3, 2):
                dve_tt(t_, t_, a_, ALU.mult, reads=[rt, ra], writes=[rt])
                dve_ts(t_, t_, 1.0 / kk, ALU.mult, reads=[rt], writes=[rt], s2=1.0, op1=ALU.add)
            em1, rem1 = tt(t_, a_, ALU.mult, rt, ra)
            mag1, rmag1 = sc_new()
            dve_ts(mag1, em1, 1.0, ALU.add, reads=[rem1], writes=[rmag1])
            magk, ck, sk, pwr, pwi_, ang, angr, ang2 = pw[0:8]
            S.op("pool", lambda e: e.memset(magk[:, :, 0:1], 1.0), writes=["magk"])
            for k in range(1, 9):
                dve_tt(magk[:, :, k], magk[:, :, k - 1], mag1, ALU.mult, reads=["magk", rmag1], writes=["magk"])
            tu, rtu = sc_new()
            dve_ts(tu, th, 1.0 / (2 * math.pi), ALU.mult, reads=[rth], writes=[rtu])
            kvb = ppc("kv", 0, 9).unsqueeze(1).broadcast_to([128, 32, 9])
            dve_tt(ang[:], tu.unsqueeze(2).broadcast_to([128, 32, 9]), kvb, ALU.mult, reads=[rtu, "pp"], writes=["ang"])

            def reduce_turns(x, rx, xi):
                S.op("dve", lambda e: e.tensor_copy(out=xi, in_=x), reads=[rx], writes=[("i", rx)])
                dve_tt(x, x, xi, ALU.subtract, reads=[rx, ("i", rx)], writes=[rx])

            reduce_turns(ang[:], "ang", pwi[:])
            act(sk[:], ang[:], AF.Sin, reads=["ang"], writes=["sk"], scale=2 * math.pi)
            act(ang2[:], ang[:], AF.Abs, reads=["ang"], writes=["ang2"])
            act(ck[:], ang2[:], AF.Sin, reads=["ang2", "pp"], writes=["ck"], scale=-2 * math.pi, bias=ppc("hpi"))
            dve_tt(pwr[:], magk[:], ck[:], ALU.mult, reads=["magk", "ck"], writes=["pwr"])
            dve_tt(pwi_[:], magk[:], sk[:], ALU.mult, reads=["magk", "sk"], writes=["pwi"])
            hh, rhh = sc_new()
            dve_ts(hh, tu, 0.5, ALU.mult, reads=[rtu], writes=[rhh])
            reduce_turns(hh, rhh, sci[:])
            shf, rshf = sc_new()
            act(shf, hh, AF.Sin, reads=[rhh], writes=[rshf], scale=2 * math.pi)
            nr, rnr = tt(em1, ck[:, :, 1], ALU.mult, rem1, "ck")
            s2_, rs2 = tt(shf, shf, ALU.mult, rshf, rshf)
            dve_stt(nr, s2_, -2.0, nr, ALU.mult, ALU.add, reads=[rs2, rnr], writes=[rnr])
            ni, rni = tt(mag1, sk[:, :, 1], ALU.mult, rmag1, "sk")
            den, rden = tt(re_, re_, ALU.mult, rre, rre)
            t2, rt2 = tt(lim, lim, ALU.mult, "sp1", "sp1")
            dve_tt(den, den, t2, ALU.add, reads=[rden, rt2], writes=[rden])
            S.op("dve", lambda e: e.reciprocal(out=den, in_=den), reads=[rden], writes=[rden])
            cr, rcr = tt(nr, re_, ALU.mult, rnr, rre)
            t3, rt3 = tt(ni, lim, ALU.mult, rni, "sp1")
            dve_tt(cr, cr, t3, ALU.add, reads=[rcr, rt3], writes=[rcr])
            dve_tt(cr, cr, den, ALU.mult, reads=[rcr, rden], writes=[rcr])
            ci, rci = tt(ni, re_, ALU.mult, rni, rre)
            t4, rt4 = tt(nr, lim, ALU.mult, rnr, "sp1")
            dve_tt(ci, ci, t4, ALU.subtract, reads=[rci, rt4], writes=[rci])
            dve_tt(ci, ci, den, ALU.mult, reads=[rci, rden], writes=[rci])
            cis, rcis = sc_new()
            dve_ts(cis, ci, ppc("sgn"), ALU.mult, reads=[rci, "pp"], writes=[rcis])
            cin, rcin = sc_new()
            dve_ts(cin, ci, ppc("nsg"), ALU.mult, reads=[rci, "pp"], writes=[rcin])
            bc16 = lambda a: a.unsqueeze(2).broadcast_to([128, 32, 16])
            Bc, Bsw, Cc, Csw = BCs[:, 0], BCs[:, 1], BCs[:, 2], BCs[:, 3]
            dve_tt(Bb[:], Bc, bc16(cr), ALU.mult, reads=["BCs", rcr], writes=["Bb"])
            dve_tt(tmpA[:], Bsw, bc16(cis), ALU.mult, reads=["BCs", rcis], writes=["tmpA"])
            dve_tt(Bb[:], Bb[:], tmpA[:], ALU.add, reads=["Bb", "tmpA"], writes=["Bb"])
            dve_tt(Bbs[:], Bsw, bc16(cr), ALU.mult, reads=["BCs", rcr], writes=["Bbs"])
            dve_tt(tmpB[:], Bc, bc16(cin), ALU.mult, reads=["BCs", rcin], writes=["tmpB"])
            dve_tt(Bbs[:], Bbs[:], tmpB[:], ALU.add, reads=["Bbs", "tmpB"], writes=["Bbs"])
            pis, rpis = pw[8], "pis"
            dve_ts(pis[:], pwi_[:], ppc("sgn"), ALU.mult, reads=["pwi", "pp"], writes=[rpis])
            for ip in range(8):
                k = 7 - ip
                dve_tt(Gt[:, :, ip, :], Bb[:], bc16(pwr[:, :, k]), ALU.mult, reads=["Bb", "pwr"], writes=[("G", ip)])
                dve_tt(tmpA[:], Bbs[:], bc16(pis[:, :, k]), ALU.mult, reads=["Bbs", rpis], writes=["tmpA"], eng="dve")
                dve_tt(Gt[:, :, ip, :], Gt[:, :, ip, :], tmpA[:], ALU.add, reads=[("G", ip), "tmpA"], writes=[("G", ip)])
            prn, rprn = pw[9], "prn"
            dve_ts(prn[:], pwr[:], ppc("nsg"), ALU.mult, reads=["pwr", "pp"], writes=[rprn])
            prs, rprs = pw[10], "prs"
            dve_ts(prs[:], pwr[:], ppc("sgn"), ALU.mult, reads=["pwr", "pp"], writes=[rprs])
            for k in range(9):
                dve_tt(Rt[:, :, k, :], Cc, bc16(prn[:, :, k]), ALU.mult, reads=["BCs", rprn], writes=[("R", k)])
                dve_tt(tmpB[:], Csw, bc16(pwi_[:, :, k]), ALU.mult, reads=["BCs", "pwi"], writes=["tmpB"], eng="dve")
                dve_tt(Rt[:, :, k, :], Rt[:, :, k, :], tmpB[:], ALU.subtract, reads=[("R", k), "tmpB"], writes=[("R", k)])
                dve_tt(R2[:, :, k, :], Csw, bc16(prs[:, :, k]), ALU.mult, reads=["BCs", rprs], writes=[("R2", k)])
                dve_tt(tmpA[:], Cc, bc16(pwi_[:, :, k]), ALU.mult, reads=["BCs", "pwi"], writes=["tmpA"], eng="dve")
                dve_tt(R2[:, :, k, :], R2[:, :, k, :], tmpA[:], ALU.subtract, reads=[("R2", k), "tmpA"], writes=[("R2", k)])
            RALL = [("R", k) for k in range(9)]
            R2ALL = [("R2", k) for k in range(9)]
            GALL = [("G", k) for k in range(8)]
            S.op("act", lambda e: e.activation(out=WY1_all[:].rearrange("p g (k c) -> p g k c", c=16), in_=Rt[:, :, 1:9, :], func=AF.Copy),
                 reads=RALL, writes=["WY1"])
            S.op("act", lambda e: e.activation(out=WY2_all[:].rearrange("p g (k c) -> p g k c", c=16), in_=R2[:, :, 1:9, :], func=AF.Copy),
                 reads=R2ALL, writes=["WY2"])
            for gb in range(8):
                pi_ = nps()

                def tr(e, gb=gb, pi_=pi_):
                    ins = None
                    for q in range(4):
                        g = gb * 4 + q
                        ins = e.transpose(ps[pi_][:, q * 128:(q + 1) * 128], Gt[:, g].rearrange("p a b -> p (a b)"), id32[:])
                    return ins
                S.op("pe", tr, reads=GALL + ["id32"], writes=[("ps", pi_)])
                pv = ps[pi_][:].rearrange("p (q m) -> p q m", m=128)
                act(WE_all[:, gb * 4:(gb + 1) * 4, :], pv, AF.Copy, reads=[("ps", pi_)], writes=[("WE", gb)])
                S.op("dve", lambda e, pv=pv, gb=gb: e.tensor_copy(out=WEs_all[:, gb * 4:(gb + 1) * 4, 0:64], in_=pv[:, :, 64:128]),
                     reads=[("ps", pi_), ("WE", gb)], writes=[("WEs", gb)])
                S.op("dve", lambda e, pv=pv, gb=gb: e.tensor_copy(out=WEs_all[:, gb * 4:(gb + 1) * 4, 64:128], in_=pv[:, :, 0:64]),
                     reads=[("ps", pi_), ("WE", gb)], writes=[("WEs", gb)])
            for gb in range(8):
                pi_ = nps()

                def kt(e, gb=gb, pi_=pi_):
                    ins = None
                    for q in range(4):
                        g = gb * 4 + q
                        ins = e.matmul(ps[pi_][0:16, q * 128:(q + 1) * 128], lhsT=Bb[:, g, :],
                                       rhs=Rt[:, g, 0:8, :].rearrange("p a b -> p (a b)"), start=True, stop=True)
                    return ins
                S.op("pe", kt, reads=RALL + ["Bb"], writes=[("ps", pi_)])
                pv = ps[pi_][0:16, :].rearrange("p (q m) -> p q m", m=128)
                act(KTs[0:16, gb * 4:(gb + 1) * 4, :], pv, AF.Copy, reads=[("ps", pi_)], writes=[("KT", gb)])
                for q in range(4):
                    g = gb * 4 + q
                    dve_stt(KTs[0:16, g, 0:16], id32[0:16, 0:16], dgs[0:16, g:g + 1], pv[:, q, 0:16], ALU.mult, ALU.add,
                            reads=["id32", "dgs", ("ps", pi_), ("KT", gb)], writes=[("KT", gb)])
            for ip in range(8):
                dma("sp", M_all[16 * ip:16 * ip + 16, :, 16 * ip:128], KTs[0:16, :, 0:(8 - ip) * 16],
                    reads=[("KT", gb) for gb in range(8)] + ["M_all"], writes=[("M", ip)])
            f_, rf = sc_new()
            dve_ts(f_, tu, 8.0, ALU.mult, reads=[rtu], writes=[rf])
            reduce_turns(f_, rf, sci[:])
            fh16 = carve(P0 + 97 * KB, [128, 32], BF16)
            S.op("dve", lambda e: e.tensor_copy(out=fh16, in_=f_), reads=[rf], writes=["fh16"])
            S.op("dve", lambda e: e.tensor_copy(out=ssc[:, 0, :], in_=fh16), reads=["fh16"], writes=["ssc0"])
            dve_tt(ssc[:, 1, :], f_, ssc[:, 0, :], ALU.subtract, reads=[rf, "ssc0"], writes=["ssc1"])
            S.op("dve", lambda e: e.tensor_copy(out=ssc[:, 2, :], in_=magk[:, :, 8]), reads=["magk"], writes=["ssc2"])
            for name, src, rr in (("WE", WE_all, [("WE", gb) for gb in range(8)]), ("M", M_all, [("M", ip) for ip in range(8)]),
                                  ("WY1", WY1_all, ["WY1"]), ("WY2", WY2_all, ["WY2"]), ("WEs", WEs_all, [("WEs", gb) for gb in range(8)])):
                if name in dbg_t:
                    dma("sp", dbg_t[name], src[:], reads=rr)
            if "ssc" in dbg_t:
                dma("sp", dbg_t["ssc"], ssc[:], reads=["ssc0", "ssc1", "ssc2"])
            S.barrier()

        UD = carve(40 * KB, [128, 4, 8, 512], BF16)
        YB = carve(72 * KB, [128, 4, L], BF16)
        WIN = carve(104 * KB, [128, 8, 2048], BF16)
        GLU = carve(104 * KB, [128, 4, 512], BF16)
        WOUT = carve(108 * KB, [128, 8, 1024], BF16)
        US = carve(136 * KB, [128, 32, 512], BF16)
        TB = 136 * KB

        if "a1" in phases:
            w_inv = w_in.rearrange("(kc p) m -> p kc m", p=128)
            for kc in range(8):
                dma("pool", WIN[:, kc, :], w_inv[:, kc, :], writes=[("WIN", kc)])
            WINR = [("WIN", kc) for kc in range(8)]
            xt = carve(TB, [128, 8, TT])
            hb = carve(TB + 16 * KB, [128, 8, TT], BF16)
            sq = [carve(TB + 24 * KB + i * KB, [128, TT], BF16) for i in range(2)]
            tmp = [carve(TB + 26 * KB + i * 2 * KB, [128, TT]) for i in range(2)]
            rstd = carve(TB + 30 * KB, [128, TT])
            srt = carve(TB + 32 * KB, [128, TT])
            cgb = [carve(TB + 34 * KB + i * 2 * KB, [128, TT]) for i in range(2)]
            cv = carve(TB + 38 * KB, [128, 4, TT + 2])
            acc = [carve(TB + 47 * KB + i * 2 * KB, [128, TT]) for i in range(2)]
            ybf = [carve(TB + 51 * KB + i * 2 * KB, [128, TT]) for i in range(2)]
            sqb = [carve(TB + 55 * KB + i * KB, [128, TT], BF16) for i in range(2)]
            rstdb = carve(TB + 57 * KB, [128, TT])
            srtb = carve(TB + 59 * KB, [128, TT])
            S.op("pool", lambda e: e.memset(cv[:, :, 0:2], 0.0), writes=["cvh"])
            xTv = xT.rearrange("(c p) t -> p c t", p=128)
            for t in range(ntiles):
                t0 = t * TT
                for c in range(8):
                    dma("sp", xt[:, c, :], xTv[:, c, t0:t0 + TT], writes=[("xt", c)])
                pst = nst()
                for c in range(8):
                    s_ = sq[c % 2]
                    act(s_, xt[:, c, :], AF.Square, reads=[("xt", c)], writes=[("sq", c % 2)])
                    S.op("pe", lambda e, s_=s_, c=c, pst=pst: e.matmul(ps[pst][:], lhsT=ones_b, rhs=s_, start=(c == 0), stop=(c == 7)),
                         reads=[("sq", c % 2), "cstb"], writes=[("ps", pst)])
                rstd_from_ps(pst, 1.0 / D, srt, rstd, "a1")
                for c in range(8):
                    tm = tmp[c % 2]
                    dve_tt(tm, xt[:, c, :], rstd, ALU.mult, reads=[("xt", c), "a1rstd"], writes=[("tmp", c % 2)])
                    act(hb[:, c, :], tm, AF.Identity, reads=[("tmp", c % 2)] + DER, writes=[("hb", c)],
                        scale=der[:, c:c + 1], bias=modT[:, c:c + 1])
                HB = [("hb", c) for c in range(8)]

                def proj(m):
                    pi_ = nps()
                    mmg(pi_, [(WIN[:, kc, m * 128:(m + 1) * 128], hb[:, kc, :]) for kc in range(8)], reads=WINR + HB)
                    return pi_
                for q in range(4):
                    pc_ = proj(8 + q)
                    cg_ = cgb[q % 2]
                    act(cg_, ps[pc_][:], AF.Copy, reads=[("ps", pc_)], writes=[("cg", q % 2)])
                    pv_ = proj(12 + q)
                    dve_tt(cv[:, q, 2:TT + 2], ps[pv_][:], cg_, ALU.mult, reads=[("ps", pv_), ("cg", q % 2)], writes=[("cv", q)])
                    ac = acc[q % 2]
                    cwc = lambda k, q=q: ppc("cw", k * 4 + q)
                    dve_ts(ac, cv[:, q, 0:TT], cwc(0), ALU.mult, reads=[("cv", q), "cvh", "pp"], writes=[("acc", q % 2)], eng="dve")
                    dve_stt(ac, cv[:, q, 1:TT + 1], cwc(1), ac, ALU.mult, ALU.add, reads=[("cv", q), "cvh", "pp", ("acc", q % 2)],
                            writes=[("acc", q % 2)], eng="dve")
                    dve_stt(ac, cv[:, q, 2:TT + 2], cwc(2), ac, ALU.mult, ALU.add, reads=[("cv", q), "pp", ("acc", q % 2)],
                            writes=[("acc", q % 2)], eng="dve")
                    S.op("pool", lambda e, q=q: e.tensor_copy(out=cv[:, q, 0:2], in_=cv[:, q, TT:TT + 2]),
                         reads=[("cv", q), ("acc", q % 2)], writes=["cvh", ("cv", q)])
                    pb_ = proj(4 + q)
                    yb_ = ybf[q % 2]
                    dve_tt(yb_, ps[pb_][:], ac, ALU.mult, reads=[("ps", pb_), ("acc", q % 2)], writes=[("ybf", q % 2)])
                    sb_ = sqb[q % 2]
                    act(sb_, yb_, AF.Square, reads=[("ybf", q % 2)], writes=[("sqb", q % 2)])
                    pn_ = nps()
                    mmg(pn_, [(bd64_b, sb_)], reads=[("sqb", q % 2), "cstb"])
                    rstd_from_ps(pn_, 1.0 / 64, srtb, rstdb, "a1b")
                    dve_stt(YB[:, q, t0:t0 + TT], yb_, ppc("goc", q), rstdb, ALU.mult, ALU.mult,
                            reads=[("ybf", q % 2), "a1brstd", "pp"], writes=[("YB", q, t)])
                for m in range(4):
                    pu_ = proj(m)
                    src = ps[pu_][:].rearrange("p (j i) -> p i j", i=8)
                    dst = UD[:, m, :, t * 64:(t + 1) * 64]
                    if m % 2 == 0:
                        act(dst, src, AF.Copy, reads=[("ps", pu_)], writes=[("UD", m, t)])
                    else:
                        S.op("dve", lambda e, dst=dst, src=src: e.tensor_copy(out=dst, in_=src), reads=[("ps", pu_)],
                             writes=[("UD", m, t)])
            if "UD" in dbg_t:
                dma("sp", dbg_t["UD"], UD[:].rearrange("p a b c -> p (a b c)"),
                    reads=[("UD", m, t) for m in range(4) for t in range(NT)])
            if "YB" in dbg_t:
                dma("sp", dbg_t["YB"], YB[:].rearrange("p a b -> p (a b)"),
                    reads=[("YB", q, t) for q in range(4) for t in range(NT)])
            S.barrier()

        if "ssm" in phases:
            USv = US[:].rearrange("p (m e) j -> p m e j", e=8)
            n_ = 0
            for i in range(8):
                for g8 in range(8):
                    dma("sp" if n_ % 2 == 0 else "act", USv[16 * i:16 * i + 16, :, g8, :], UD[16 * g8:16 * g8 + 16, :, i, :],
                        writes=[("US", i, g8)])
                    n_ += 1
            SB0 = 168 * KB
            angb = [carve(SB0 + i * 2 * KB, [128, 512]) for i in range(2)]
            angi = carve(SB0 + 4 * KB, [128, 512], I32)
            cosb = [carve(SB0 + 6 * KB + i * 2 * KB, [128, 512]) for i in range(2)]
            sinb = [carve(SB0 + 10 * KB + i * 2 * KB, [128, 512]) for i in range(2)]
            tA = carve(SB0 + 14 * KB, [128, 512])
            tB = carve(SB0 + 16 * KB, [128, 512])
            Ep = carve(SB0 + 18 * KB, [128, 512])
            zb = carve(SB0 + 20 * KB, [128, 512])
            P1 = [carve(SB0 + 22 * KB + i * 1028, [128, 513], BF16) for i in range(2)]
            P2 = [carve(SB0 + 22 * KB + 2056 + i * 1028, [128, 513], BF16) for i in range(2)]
            ab2 = carve(SB0 + 27 * KB, [128, 512])
            for i in range(2):
                S.op("pool", lambda e, i=i: e.memset(P1[i][:, 0:1], 0.0), writes=[("P1", i)])
                S.op("pool", lambda e, i=i: e.memset(P2[i][:, 0:1], 0.0), writes=[("P2", i)])
            for g in range(32):
                b = g % 2
                usg = US[:, g, :]
                USR = [("US", i, g % 8) for i in range(8)]
                q1 = nps()
                mmg(q1, [(WE_all[:, g, :], usg)], reads=USR)
                q2 = nps()
                mmg(q2, [(WEs_all[:, g, :], usg)], reads=USR)
                an = angb[b]
                dve_ts(an, iota[:], ssc[:, 0, g:g + 1], ALU.mult, reads=["iota"], writes=[("ang", b)])
                S.op("dve", lambda e, an=an: e.tensor_copy(out=angi, in_=an), reads=[("ang", b)], writes=["angi"])
                dve_tt(an, an, angi, ALU.subtract, reads=[("ang", b), "angi"], writes=[("ang", b)])
                dve_stt(an, iota[:], ssc[:, 1, g:g + 1], an, ALU.mult, ALU.add, reads=[("ang", b), "iota"], writes=[("ang", b)])
                S.op("dve", lambda e, an=an: e.tensor_copy(out=angi, in_=an), reads=[("ang", b)], writes=["angi"])
                dve_tt(an, an, angi, ALU.subtract, reads=[("ang", b), "angi"], writes=[("ang", b)])
                act(sinb[b], an, AF.Sin, reads=[("ang", b), "pp"], writes=[("sin", b)], scale=ppc("tpn"))
                act(ab2, an, AF.Abs, reads=[("ang", b)], writes=["ab2"])
                act(cosb[b], ab2, AF.Sin, reads=["ab2", "pp"], writes=[("cos", b)], scale=-2 * math.pi, bias=ppc("hpi"))
                dve_tt(tA, ps[q1][:], cosb[b], ALU.mult, reads=[("ps", q1), ("cos", b)], writes=["tA"])
                dve_tt(tB, ps[q2][:], sinb[b], ALU.mult, reads=[("ps", q2), ("sin", b)], writes=["tB"])
                dve_tt(Ep, tA, tB, ALU.add, reads=["tA", "tB"], writes=["Ep"], eng="dve")
                r8b = ssc[:, 2, g:g + 1].broadcast_to([128, 512])
                S.op("dve", lambda e, r8b=r8b: e.tensor_tensor_scan(out=zb, data0=r8b, data1=Ep, initial=0.0, op0=ALU.mult, op1=ALU.add),
                     reads=["Ep", "ssc"], writes=["zb"])
                dve_tt(P1[b][:, 1:513], zb, cosb[b], ALU.mult, reads=["zb", ("cos", b)], writes=[("P1", b)], eng="dve")
                dve_tt(P2[b][:, 1:513], zb, sinb[b], ALU.mult, reads=["zb", ("sin", b)], writes=[("P2", b)])
                py = nps()
                mmg(py, [(M_all[:, g, :], usg), (WY1_all[:, g, :], P1[b][:, 0:512]), (WY2_all[:, g, :], P2[b][:, 0:512])],
                    reads=USR + [("P1", b), ("P2", b)])
                S.op("act", lambda e, py=py, usg=usg: e.activation(out=usg, in_=ps[py][:], func=AF.Gelu_apprx_tanh),
                     reads=[("ps", py)] + USR, writes=[("ZS", g)])
            if "ZS" in dbg_t:
                dma("sp", dbg_t["ZS"], US[:].rearrange("p a b -> p (a b)"), reads=[("ZS", g) for g in range(32)])
            n_ = 0
            for i in range(8):
                for g8 in range(8):
                    dma("sp" if n_ % 2 == 0 else "act", UD[16 * g8:16 * g8 + 16, :, i, :], USv[16 * i:16 * i + 16, :, g8, :],
                        reads=[("ZS", m * 8 + g8) for m in range(4)], writes=[("ZF", i, g8)])
                    n_ += 1
            S.barrier()

        if "a2" in phases:
            glv = glu_w.rearrange("(kc p) m -> p kc m", p=128)
            wov = w_out.rearrange("(kc p) m -> p kc m", p=128)
            for kc in range(4):
                dma("pool", GLU[:, kc, :], glv[:, kc, :], writes=[("GLU", kc)])
            for kc in range(8):
                dma("pool", WOUT[:, kc, :], wov[:, kc, :], writes=[("WOUT", kc)])
            TB2 = 124 * KB
            xt = carve(TB2, [128, 8, TT])
            zt = carve(TB2 + 16 * KB, [128, 4, TT], BF16)
            sig = [carve(TB2 + 20 * KB + i * 2 * KB, [128, TT]) for i in range(2)]
            ya = [carve(TB2 + 24 * KB + i * 2 * KB, [128, TT]) for i in range(2)]
            sqa = [carve(TB2 + 28 * KB + i * KB, [128, TT], BF16) for i in range(2)]
            yan = carve(TB2 + 30 * KB, [128, 4, TT], BF16)
            rstd = carve(TB2 + 34 * KB, [128, TT])
            srt = carve(TB2 + 36 * KB, [128, TT])
            ysb = carve(TB2 + 38 * KB, [128, 8, TT])
            sq = [carve(TB2 + 54 * KB + i * KB, [128, TT], BF16) for i in range(2)]
            tmp = [carve(TB2 + 56 * KB + i * 2 * KB, [128, TT]) for i in range(2)]
            rstd2 = carve(TB2 + 60 * KB, [128, TT])
            srt2 = carve(TB2 + 62 * KB, [128, TT])
            xTv = xT.rearrange("(c p) t -> p c t", p=128)
            x1v = x1s.rearrange("(c p) t -> p c t", p=128)
            for t in range(ntiles):
                t0 = t * TT
                for c in range(8):
                    dma("sp", xt[:, c, :], xTv[:, c, t0:t0 + TT], writes=[("xt", c)])
                for m in range(4):
                    src = UD[:, m, :, t * 64:(t + 1) * 64].rearrange("p i j -> p j i")
                    dst = zt[:, m, :].rearrange("p (j i) -> p j i", i=8)
                    S.op("pool", lambda e, src=src, dst=dst: e.tensor_copy(out=dst, in_=src),
                         reads=[("ZF", i, g8) for i in range(8) for g8 in range(8)], writes=[("zt", m)])
                ZT = [("zt", m) for m in range(4)]
                for m in range(4):
                    pg = nps()
                    mmg(pg, [(GLU[:, kc, m * 128:(m + 1) * 128], zt[:, kc, :]) for kc in range(4)],
                        reads=ZT + [("GLU", kc) for kc in range(4)])
                    act(sig[m % 2], ps[pg][:], AF.Sigmoid, reads=[("ps", pg), "pp"], writes=[("sig", m % 2)], bias=ppc("glub", m))
                    dve_tt(ya[m % 2], zt[:, m, :], sig[m % 2], ALU.mult, reads=[("zt", m), ("sig", m % 2)], writes=[("ya", m % 2)])
                    act(sqa[m % 2], ya[m % 2], AF.Square, reads=[("ya", m % 2)], writes=[("sqa", m % 2)])
                    pn_ = nps()
                    mmg(pn_, [(bd16_b, sqa[m % 2])], reads=[("sqa", m % 2), "cstb"])
                    rstd_from_ps(pn_, 1.0 / 16, srt, rstd, "a2h")
                    dve_stt(yan[:, m, :], ya[m % 2], ppc("gos", m), rstd, ALU.mult, ALU.mult,
                            reads=[("ya", m % 2), "a2hrstd", "pp"], writes=[("yan", m)])
                YAN = [("yan", m) for m in range(4)]
                pst = nst()
                for mo in range(8):
                    po = nps()
                    pairs = [(WOUT[:, kc, mo * 128:(mo + 1) * 128], yan[:, kc, :]) for kc in range(4)]
                    pairs += [(WOUT[:, 4 + kc, mo * 128:(mo + 1) * 128], YB[:, kc, t0:t0 + TT]) for kc in range(4)]
                    mmg(po, pairs, reads=YAN + [("WOUT", kc) for kc in range(8)])
                    S.op("dve", lambda e, mo=mo, po=po: e.tensor_copy(out=ysb[:, mo, :], in_=ps[po][:]), reads=[("ps", po)],
                         writes=[("ysb", mo)])
                    act(sq[mo % 2], ysb[:, mo, :], AF.Square, reads=[("ysb", mo)], writes=[("sq", mo % 2)])
                    S.op("pe", lambda e, mo=mo, pst=pst: e.matmul(ps[pst][:], lhsT=ones_b, rhs=sq[mo % 2], start=(mo == 0), stop=(mo == 7)),
                         reads=[("sq", mo % 2), "cstb"], writes=[("ps", pst)])
                rstd_from_ps(pst, 1.0 / D, srt2, rstd2, "a2p")
                for c in range(8):
                    tm = tmp[c % 2]
                    dve_tt(tm, ysb[:, c, :], rstd2, ALU.mult, reads=[("ysb", c), "a2prstd"], writes=[("tmp", c % 2)], eng="dve")
                    dve_stt(xt[:, c, :], tm, der[:, 8 + c:9 + c], xt[:, c, :], ALU.mult, ALU.add,
                            reads=[("tmp", c % 2), ("xt", c)] + DER, writes=[("xt", c)])
                    dma("sp", x1v[:, c, t0:t0 + TT], xt[:, c, :], reads=[("xt", c)], writes=[("x1s", t, c)])
            S.barrier()

        if "ffn" in phases:
            WUP = carve(0, [128, 8, 2 * DFF], BF16)
            WDN = carve(88 * KB, [128, NF, D], BF16)
            wuv = w_up.rearrange("(kc p) m -> p kc m", p=128)
            wdv = w_down.rearrange("(kc p) m -> p kc m", p=128)
            for kc in range(8):
                for hh in range(2):
                    for q4 in range(2):
                        c0 = hh * DFF + q4 * (DFF // 2)
                        dma("pool", WUP[:, kc, c0:c0 + DFF // 2], wuv[:, kc, c0:c0 + DFF // 2], writes=[("WUP", kc, hh, q4)])
            for kc in range(NF):
                dma("pool", WDN[:, kc, :], wdv[:, kc, :], writes=[("WDN", kc)])
            WUPR = [("WUP", kc, hh, q4) for kc in range(8) for hh in range(2) for q4 in range(2)]
            WDNR = [("WDN", kc) for kc in range(NF)]
            TB3 = 132 * KB
            xt = carve(TB3, [128, 8, TT])
            h2 = carve(TB3 + 16 * KB, [128, 8, TT], BF16)
            actb = carve(TB3 + 24 * KB, [128, NF, TT], BF16)
            sq = [carve(TB3 + 46 * KB + i * KB, [128, TT], BF16) for i in range(2)]
            tmp = [carve(TB3 + 48 * KB + i * 2 * KB, [128, TT]) for i in range(2)]
            rstd = carve(TB3 + 52 * KB, [128, TT])
            srt = None
            upb = [carve(TB3 + 54 * KB + i * 2064, [128, TT + 2]) for i in range(2)]
            acc = [carve(TB3 + 54 * KB + 4128 + i * 2 * KB, [128, TT]) for i in range(2)]
            xr = [carve(TB3 + 54 * KB + 4128 + 4 * KB + i * 2 * KB, [128, TT]) for i in range(2)]
            x1v = x1s.rearrange("(c p) t -> p c t", p=128)
            oTv = outT.rearrange("(c p) t -> p c t", p=128)
            src_v = x1v if "a2" in phases else xT.rearrange("(c p) t -> p c t", p=128)
            for t in range(ntiles):
                t0 = t * TT
                for c in range(8):
                    dma("sp", xt[:, c, :], src_v[:, c, t0:t0 + TT], reads=[("x1s", t, c)], writes=[("xt", c)])
                pst = nst()
                for c in range(8):
                    s_ = sq[c % 2]
                    act(s_, xt[:, c, :], AF.Square, reads=[("xt", c)], writes=[("sq", c % 2)])
                    S.op("pe", lambda e, s_=s_, c=c, pst=pst: e.matmul(ps[pst][:], lhsT=ones_b, rhs=s_, start=(c == 0), stop=(c == 7)),
                         reads=[("sq", c % 2), "cstb"], writes=[("ps", pst)])
                rstd_from_ps(pst, 1.0 / D, srt, rstd, "b")
                for c in range(8):
                    tm = tmp[c % 2]
                    dve_tt(tm, xt[:, c, :], rstd, ALU.mult, reads=[("xt", c), "brstd"], writes=[("tmp", c % 2)])
                    act(h2[:, c, :], tm, AF.Identity, reads=[("tmp", c % 2)] + DER, writes=[("h2", c)],
                        scale=der[:, 16 + c:17 + c], bias=modT[:, 24 + c:25 + c])
                H2 = [("h2", c) for c in range(8)]
                if upto < 2:
                    continue
                for f in range(NF if upto >= 3 else 1):
                    hid = []
                    for which in range(2):
                        ch = which * NF + f
                        pu = nps()
                        mmg(pu, [(WUP[:, kc, ch * 128:(ch + 1) * 128], h2[:, kc, :]) for kc in range(8)], reads=WUPR + H2)
                        ub = upb[which]
                        S.op("pool", lambda e, ub=ub, ch=ch: e.tensor_copy(out=ub[:, 0:2], in_=halo[:, ch, :]),
                             reads=["halo", ("acc", which)], writes=[("uph", which)])
                        act(ub[:, 2:TT + 2], ps[pu][:], AF.Copy, reads=[("ps", pu), ("acc", which)], writes=[("upb", which)])
                        S.op("pool", lambda e, ub=ub, ch=ch: e.tensor_copy(out=halo[:, ch, :], in_=ub[:, TT:TT + 2]),
                             reads=[("upb", which)], writes=["halo"])
                        ac = acc[which]
                        fw = lambda k, ch=ch: ppc("fcw", k * 44 + ch)
                        eng = "dve"
                        dve_ts(ac, ub[:, 0:TT], fw(0), ALU.mult, reads=[("upb", which), ("uph", which), "pp"], writes=[("acc", which)], eng=eng)
                        dve_stt(ac, ub[:, 1:TT + 1], fw(1), ac, ALU.mult, ALU.add, reads=[("upb", which), ("uph", which), "pp", ("acc", which)],
                                writes=[("acc", which)], eng=eng)
                        dve_stt(ac, ub[:, 2:TT + 2], fw(2), ac, ALU.mult, ALU.add, reads=[("upb", which), "pp", ("acc", which)],
                                writes=[("acc", which)], eng=eng)
                        hid.append(ac)
                    act(hid[0], hid[0], AF.Silu, reads=[("acc", 0)], writes=[("acc", 0)])
                    dve_tt(actb[:, f, :], hid[0], hid[1], ALU.mult, reads=[("acc", 0), ("acc", 1)], writes=[("actb", f)])
                ACTB = [("actb", f) for f in range(NF)]
                if upto < 4:
                    continue
                pst = nst()
                for mo in range(8):
                    po = nps()
                    mmg(po, [(WDN[:, f, mo * 128:(mo + 1) * 128], actb[:, f, :]) for f in range(NF)], reads=ACTB + WDNR)
                    S.op("dve", lambda e, mo=mo, po=po: e.tensor_copy(out=xt[:, mo, :], in_=ps[po][:]), reads=[("ps", po)] + H2,
                         writes=[("xt", mo)])
                    act(sq[mo % 2], xt[:, mo, :], AF.Square, reads=[("xt", mo)], writes=[("sq", mo % 2)])
                    S.op("pe", lambda e, mo=mo, pst=pst: e.matmul(ps[pst][:], lhsT=ones_b, rhs=sq[mo % 2], start=(mo == 0), stop=(mo == 7)),
                         reads=[("sq", mo % 2), "cstb"], writes=[("ps", pst)])
                rstd_from_ps(pst, 1.0 / D, srt, rstd, "b")
                if upto < 5:
                    continue
                for c in range(8):
                    tm = tmp[c % 2]
                    xr_ = xr[c % 2]
                    dma("sp", xr_, src_v[:, c, t0:t0 + TT], reads=[("x1s", t, c)], writes=[("xr", c % 2)])
                    dve_tt(tm, xt[:, c, :], rstd, ALU.mult, reads=[("xt", c), "brstd"], writes=[("tmp", c % 2)], eng="dve")
                    dve_stt(xr_, tm, der[:, 24 + c:25 + c], xr_, ALU.mult, ALU.add,
                            reads=[("tmp", c % 2), ("xr", c % 2)] + DER, writes=[("xr", c % 2)])
                    dma("sp", oTv[:, c, t0:t0 + TT], xr_, reads=[("xr", c % 2)], writes=[("out", t, c)])
        S.barrier()
        S.emit()
    return nc


_CACHE = {}


def kernel(**inp):
    inp = {k: np.asarray(v) for k, v in inp.items()}
    B = inp["x"].shape[0]
    cst, iota = make_consts()
    sp1, BC, dg = pack_ssm(inp)
    shared = {
        "w_ada": np.ascontiguousarray(inp["w_ada"][0], dtype=np.float32),
        "w_in": np.ascontiguousarray(inp["w_in"][0], dtype=np.float32),
        "glu_w": np.ascontiguousarray(inp["glu_w"][0], dtype=np.float32),
        "w_out": np.ascontiguousarray(inp["w_out"][0], dtype=np.float32),
        "w_up": np.ascontiguousarray(inp["w_up"][0], dtype=np.float32),
        "w_down": np.ascontiguousarray(inp["w_down"][0], dtype=np.float32),
        "sp1": sp1, "BC": BC, "dg": dg, "cst": cst, "iota": iota,
    }
    in_maps = []
    for b in range(B):
        m = dict(shared)
        m["xT"] = np.ascontiguousarray(inp["x"][b].T, dtype=np.float32)
        m["pp"] = pack_pp(inp, b)
        in_maps.append(m)
    if "nc" not in _CACHE:
        _CACHE["nc"] = build_nc()
    nc = _CACHE["nc"]
    res = run_bass_kernel_spmd(nc, in_maps, core_ids=list(range(B)))
    out = np.stack([np.asarray(res.results[b]["outT"]).T for b in range(B)], axis=0)
    return np.ascontiguousarray(out, dtype=np.float32)
```
